# Optimizing a Trainium2 kernel written in Bass

```python
import math
import jax, jax.numpy as jnp
from jax import lax
import numpy as np

D_MODEL = 1024
BATCH = 8
SEQ = 4096
DEPTH = 2
DEC_BATCH = 128
DEC_SEQ = 1
PAST_LEN = 16384
PAGE_SIZE = 128

HEAD_DIM = 64
SSM_GROUP = 16
SSM_WIDTH = 512
SSM_GROUPS = SSM_WIDTH // SSM_GROUP
SSM_STATE = 64
SWA_WINDOW = 128
SWA_Q_HEADS = 8
SWA_KV_HEADS = 2
DIL_PAIRS = ((128, 1), (512, 4), (2048, 16))
DIL_HEADS = 4
BAND = 128
N_BRANCH = 3
D_FF = 4 * D_MODEL
PLE_DIM = 256
EPS = 1e-6

SWA_Q_W = SWA_Q_HEADS * HEAD_DIM
SWA_KV_W = SWA_KV_HEADS * HEAD_DIM
DIL_W = DIL_HEADS * HEAD_DIM
SPLITS = (SSM_WIDTH, SWA_Q_W, SWA_KV_W, SWA_KV_W) + (DIL_W,) * (3 * len(DIL_PAIRS)) + (D_MODEL,) * N_BRANCH
D_IN = sum(SPLITS)

kernel_name = 'hybrid_s5_swa_dilated_decoder_step'


def rmsnorm(x, g):
    xf = x.astype(jnp.float32)
    y = xf * lax.rsqrt(jnp.mean(xf * xf, axis=-1, keepdims=True) + EPS)
    return (y * g.astype(jnp.float32)).astype(x.dtype)


def cplx_affine_combine(e1, e2):
    a1r, a1i, b1r, b1i = e1
    a2r, a2i, b2r, b2i = e2
    return (a2r * a1r - a2i * a1i,
            a2r * a1i + a2i * a1r,
            a2r * b1r - a2i * b1i + b2r,
            a2r * b1i + a2i * b1r + b2i)


def ssm_branch(u, h0, lw):
    n, t, _ = u.shape
    f32 = jnp.float32
    a_re, a_im = lw['a_re'].astype(f32), lw['a_im'].astype(f32)
    dt = jnp.exp(lw['log_dt'].astype(f32))[:, None]
    mag = jnp.exp(a_re * dt)
    lam_r, lam_i = mag * jnp.cos(a_im * dt), mag * jnp.sin(a_im * dt)
    den = a_re * a_re + a_im * a_im
    zr = ((lam_r - 1.0) * a_re + lam_i * a_im) / den
    zi = (lam_i * a_re - (lam_r - 1.0) * a_im) / den
    b_re, b_im = lw['b_re'].astype(f32), lw['b_im'].astype(f32)
    bb_r = zr[..., None] * b_re - zi[..., None] * b_im
    bb_i = zr[..., None] * b_im + zi[..., None] * b_re
    ug = jnp.swapaxes(u.astype(f32).reshape(n, t, SSM_GROUPS, SSM_GROUP), 0, 1)
    x_r = jnp.einsum('tngh,gph->tngp', ug, bb_r)
    x_i = jnp.einsum('tngh,gph->tngp', ug, bb_i)
    h0r, h0i = h0[..., 0].astype(f32), h0[..., 1].astype(f32)
    x_r = x_r.at[0].add(lam_r * h0r - lam_i * h0i)
    x_i = x_i.at[0].add(lam_r * h0i + lam_i * h0r)
    shape = (t, 1, SSM_GROUPS, SSM_STATE)
    _, _, s_r, s_i = lax.associative_scan(
        cplx_affine_combine,
        (jnp.broadcast_to(lam_r, shape), jnp.broadcast_to(lam_i, shape), x_r, x_i), axis=0)
    y = (jnp.einsum('tngp,ghp->tngh', s_r, lw['c_re'].astype(f32))
         - jnp.einsum('tngp,ghp->tngh', s_i, lw['c_im'].astype(f32)))
    y = y + lw['d'].astype(f32).reshape(SSM_GROUPS, SSM_GROUP) * ug
    y = jnp.swapaxes(y, 0, 1).reshape(n, t, SSM_WIDTH)
    g = jax.nn.gelu(y)
    out = g * jax.nn.sigmoid(g @ lw['w_glu'].astype(f32) + lw['b_glu'].astype(f32))
    return out.astype(u.dtype), jnp.stack([s_r[-1], s_i[-1]], axis=-1)


def banded_attention(q, k, v, sink):
    n, l, hq, dh = q.shape
    hkv = k.shape[2]
    grp = hq // hkv
    nb = -(-l // BAND)
    padw = ((0, 0), (0, nb * BAND - l), (0, 0), (0, 0))
    q, k, v = jnp.pad(q, padw), jnp.pad(k, padw), jnp.pad(v, padw)
    qb = q.reshape(n, nb, BAND, hkv, grp, dh)

    def prev_and_cur(a):
        ab = a.reshape(n, nb, BAND, hkv, dh)
        prev = jnp.concatenate([jnp.zeros_like(ab[:, :1]), ab[:, :-1]], axis=1)
        return jnp.concatenate([prev, ab], axis=2)

    kb, vb = prev_and_cur(k), prev_and_cur(v)
    s = jnp.einsum('nbqhgd,nbkhd->nbhgqk', qb, kb, preferred_element_type=jnp.float32) * (dh ** -0.5)
    qi = jnp.arange(BAND)[:, None]
    kj = jnp.arange(2 * BAND)[None, :]
    dist = qi + BAND - kj
    kpos = (jnp.arange(nb)[:, None, None] - 1) * BAND + kj[None]
    mask = ((dist >= 0) & (dist <= BAND))[None] & (kpos >= 0)
    s = jnp.where(mask[None, :, None, None], s, -jnp.inf)
    lse = jax.nn.logsumexp(s, axis=-1)
    if sink is not None:
        lse = jnp.logaddexp(lse, sink.astype(jnp.float32).reshape(hkv, grp, 1))
    p = jnp.exp(s - lse[..., None]).astype(v.dtype)
    o = jnp.einsum('nbhgqk,nbkhd->nbqhgd', p, vb).reshape(n, nb * BAND, hq, dh)[:, :l]
    lse = lse.transpose(0, 1, 4, 2, 3).reshape(n, nb * BAND, hq)[:, :l]
    return o, lse


def gathered_attention(q, k_all, v_all, idx, valid, sink):
    n, t, hq, dh = q.shape
    hkv = k_all.shape[2]
    grp = hq // hkv
    kg, vg = k_all[:, idx], v_all[:, idx]
    qg = q.reshape(n, t, hkv, grp, dh)
    s = jnp.einsum('nthgd,ntkhd->nthgk', qg, kg, preferred_element_type=jnp.float32) * (dh ** -0.5)
    s = jnp.where(valid[None, :, None, None, :], s, -jnp.inf)
    lse = jax.nn.logsumexp(s, axis=-1)
    if sink is not None:
        lse = jnp.logaddexp(lse, sink.astype(jnp.float32).reshape(hkv, grp))
    p = jnp.exp(s - lse[..., None]).astype(v_all.dtype)
    o = jnp.einsum('nthgk,ntkhd->nthgd', p, vg).reshape(n, t, hq, dh)
    return o, lse.reshape(n, t, hq)


def window_index(window, dilation, buf_len, n_new):
    offs = jnp.arange(window // dilation + 1) * dilation
    idx = buf_len + jnp.arange(n_new)[:, None] - offs[None, :]
    return jnp.maximum(idx, 0), idx >= 0


def to_strided(a, d):
    n, l = a.shape[:2]
    rest = a.shape[2:]
    return jnp.swapaxes(a.reshape(n, l // d, d, *rest), 1, 2).reshape(n * d, l // d, *rest)


def from_strided(a, d, n):
    ld = a.shape[1]
    rest = a.shape[2:]
    return jnp.swapaxes(a.reshape(n, d, ld, *rest), 1, 2).reshape(n, ld * d, *rest)


def token_mixer(h, lw, cache):
    n, t, _ = h.shape
    z = jnp.einsum('btd,de->bte', h, lw['w_in'])
    offs = [int(o) for o in np.cumsum(SPLITS)[:-1]]
    pieces = jnp.split(z, offs, axis=-1)
    u = pieces[0]
    g_a, g_b, g_c = pieces[-N_BRANCH:]
    heads = lambda a: a.reshape(n, t, -1, HEAD_DIM)
    q_s, k_s, v_s = heads(pieces[1]), heads(pieces[2]), heads(pieces[3])

    h0 = jnp.zeros((n, SSM_GROUPS, SSM_STATE, 2), jnp.float32) if cache is None else cache['ssm']
    o_a, ssm_new = ssm_branch(u, h0, lw)

    if cache is None:
        o_b, _ = banded_attention(q_s, k_s, v_s, lw['sinks'])
        keep = min(SWA_WINDOW, t)
        swa_new = jnp.stack([k_s[:, t - keep:], v_s[:, t - keep:]], axis=2)
    else:
        buf = cache['swa']
        idx, valid = window_index(SWA_WINDOW, 1, buf.shape[1], t)
        o_b, _ = gathered_attention(q_s, jnp.concatenate([buf[:, :, 0], k_s], axis=1),
                                    jnp.concatenate([buf[:, :, 1], v_s], axis=1), idx, valid, lw['sinks'])
        swa_new = jnp.stack([k_s, v_s], axis=2)

    outs, lses, dil_new = [], [], []
    for gi, (win, dil) in enumerate(DIL_PAIRS):
        q_d, k_d, v_d = (heads(a) for a in pieces[4 + 3 * gi: 7 + 3 * gi])
        if cache is None:
            o, lse = banded_attention(to_strided(q_d, dil), to_strided(k_d, dil), to_strided(v_d, dil), None)
            o, lse = from_strided(o, dil, n), from_strided(lse, dil, n)
            keep = min(win, t)
            dil_new.append(jnp.stack([k_d[:, t - keep:], v_d[:, t - keep:]], axis=2))
        else:
            buf = cache['dil'][gi]
            idx, valid = window_index(win, dil, buf.shape[1], t)
            o, lse = gathered_attention(q_d, jnp.concatenate([buf[:, :, 0], k_d], axis=1),
                                        jnp.concatenate([buf[:, :, 1], v_d], axis=1), idx, valid, None)
            dil_new.append(jnp.stack([k_d, v_d], axis=2))
        outs.append(o)
        lses.append(lse)
    w = jax.nn.softmax(jnp.stack(lses), axis=0)[..., None]
    o_c = jnp.sum(w * jnp.stack(outs).astype(jnp.float32), axis=0).astype(h.dtype)

    y_a = o_a @ lw['w_branch_a']
    y_b = o_b.reshape(n, t, SWA_Q_W) @ lw['w_branch_b']
    y_c = o_c.reshape(n, t, DIL_W) @ lw['w_branch_c']
    merged = jax.nn.sigmoid(g_a) * y_a + jax.nn.sigmoid(g_b) * y_b + jax.nn.sigmoid(g_c) * y_c
    return merged @ lw['w_out'], (swa_new, dil_new[0], dil_new[1], dil_new[2], ssm_new)


def decoder_layer(x, p_l, lw, cache):
    mix, st = token_mixer(rmsnorm(x, lw['g_mix']), lw, cache)
    x = x + mix
    hm = rmsnorm(x, lw['g_mlp'])
    x = x + jnp.square(jax.nn.relu(hm @ lw['w_up'])) @ lw['w_down']
    gate = jax.nn.sigmoid(rmsnorm(x, lw['g_ple']) @ lw['w_ple_gate'])
    x = x + gate * (p_l @ lw['w_ple_proj'])
    return x, st


def setup_inputs(seed: int = 0) -> dict:
    key = jax.random.key(seed)
    ks = jax.random.split(key, 40)
    f32 = jnp.float32
    nrm = lambda i, shape, scale: scale * jax.random.normal(ks[i], shape, f32)
    L = DEPTH
    G, P, H = SSM_GROUPS, SSM_STATE, SSM_GROUP
    kv_shape = lambda win, nh: (L, DEC_BATCH, min(win, PAST_LEN), 2, nh, HEAD_DIM)
    return {
        'x_prompt': nrm(0, (BATCH, SEQ, D_MODEL), 1.0),
        'x_sample': nrm(1, (DEC_BATCH, DEC_SEQ, D_MODEL), 1.0),
        'cache_swa_kv': nrm(2, kv_shape(SWA_WINDOW, SWA_KV_HEADS), 1.0),
        'cache_dil_d1_kv': nrm(3, kv_shape(DIL_PAIRS[0][0], DIL_HEADS), 1.0),
        'cache_dil_d4_kv': nrm(4, kv_shape(DIL_PAIRS[1][0], DIL_HEADS), 1.0),
        'cache_dil_d16_kv': nrm(5, kv_shape(DIL_PAIRS[2][0], DIL_HEADS), 1.0),
        'state_ssm': nrm(6, (L, DEC_BATCH, G, P, 2), 0.1),
        'p_prompt': nrm(7, (L, BATCH, SEQ, PLE_DIM), 1.0),
        'p_sample': nrm(8, (L, DEC_BATCH, DEC_SEQ, PLE_DIM), 1.0),
        'w_in': nrm(9, (L, D_MODEL, D_IN), D_MODEL ** -0.5),
        'g_mix': 1.0 + nrm(10, (L, D_MODEL), 0.01),
        'ssm_a_re': -0.5 + nrm(11, (L, G, P), 0.01),
        'ssm_a_im': jnp.pi * jnp.arange(P, dtype=f32) + nrm(12, (L, G, P), 0.01),
        'ssm_log_dt': jax.random.uniform(ks[13], (L, G), f32, math.log(1e-3), math.log(1e-1)),
        'ssm_b_re': nrm(14, (L, G, P, H), (2 * H) ** -0.5),
        'ssm_b_im': nrm(15, (L, G, P, H), (2 * H) ** -0.5),
        'ssm_c_re': nrm(16, (L, G, H, P), P ** -0.5),
        'ssm_c_im': nrm(17, (L, G, H, P), P ** -0.5),
        'ssm_d': nrm(18, (L, SSM_WIDTH), 1.0),
        'w_glu': nrm(19, (L, SSM_WIDTH, SSM_WIDTH), SSM_WIDTH ** -0.5),
        'b_glu': nrm(20, (L, SSM_WIDTH), 0.01),
        'attn_sinks': nrm(21, (L, SWA_Q_HEADS), 0.5),
        'w_branch_a': nrm(22, (L, SSM_WIDTH, D_MODEL), SSM_WIDTH ** -0.5),
        'w_branch_b': nrm(23, (L, SWA_Q_W, D_MODEL), SWA_Q_W ** -0.5),
        'w_branch_c': nrm(24, (L, DIL_W, D_MODEL), DIL_W ** -0.5),
        'w_out': nrm(25, (L, D_MODEL, D_MODEL), D_MODEL ** -0.5),
        'g_mlp': 1.0 + nrm(26, (L, D_MODEL), 0.01),
        'w_up': nrm(27, (L, D_MODEL, D_FF), D_MODEL ** -0.5),
        'w_down': nrm(28, (L, D_FF, D_MODEL), D_FF ** -0.5),
        'g_ple': 1.0 + nrm(29, (L, D_MODEL), 0.01),
        'w_ple_gate': nrm(30, (L, D_MODEL, D_MODEL), D_MODEL ** -0.5),
        'w_ple_proj': nrm(31, (L, PLE_DIM, D_MODEL), PLE_DIM ** -0.5),
        'g_final': 1.0 + nrm(32, (D_MODEL,), 0.01),
    }


def reference(x_prompt, x_sample, cache_swa_kv, cache_dil_d1_kv, cache_dil_d4_kv, cache_dil_d16_kv, state_ssm,
              p_prompt, p_sample, w_in, g_mix, ssm_a_re, ssm_a_im, ssm_log_dt, ssm_b_re, ssm_b_im, ssm_c_re,
              ssm_c_im, ssm_d, w_glu, b_glu, attn_sinks, w_branch_a, w_branch_b, w_branch_c, w_out, g_mlp,
              w_up, w_down, g_ple, w_ple_gate, w_ple_proj, g_final):
    yp, ys = x_prompt, x_sample
    st_p, st_s = [], []
    for l in range(DEPTH):
        lw = {'w_in': w_in[l], 'g_mix': g_mix[l], 'a_re': ssm_a_re[l], 'a_im': ssm_a_im[l],
              'log_dt': ssm_log_dt[l], 'b_re': ssm_b_re[l], 'b_im': ssm_b_im[l], 'c_re': ssm_c_re[l],
              'c_im': ssm_c_im[l], 'd': ssm_d[l], 'w_glu': w_glu[l], 'b_glu': b_glu[l], 'sinks': attn_sinks[l],
              'w_branch_a': w_branch_a[l], 'w_branch_b': w_branch_b[l], 'w_branch_c': w_branch_c[l],
              'w_out': w_out[l], 'g_mlp': g_mlp[l], 'w_up': w_up[l], 'w_down': w_down[l], 'g_ple': g_ple[l],
              'w_ple_gate': w_ple_gate[l], 'w_ple_proj': w_ple_proj[l]}
        cache_l = {'swa': cache_swa_kv[l],
                   'dil': (cache_dil_d1_kv[l], cache_dil_d4_kv[l], cache_dil_d16_kv[l]),
                   'ssm': state_ssm[l]}
        yp, sp = decoder_layer(yp, p_prompt[l], lw, None)
        ys, ss = decoder_layer(ys, p_sample[l], lw, cache_l)
        st_p.append(sp)
        st_s.append(ss)
    yp = rmsnorm(yp, g_final)
    ys = rmsnorm(ys, g_final)
    stk = lambda sts, i: jnp.stack([s[i] for s in sts])
    return (yp, ys,
            stk(st_p, 0), stk(st_p, 1), stk(st_p, 2), stk(st_p, 3), stk(st_p, 4),
            stk(st_s, 0), stk(st_s, 1), stk(st_s, 2), stk(st_s, 3), stk(st_s, 4))
```

```python
import numpy as np
import concourse.bass as bass
import concourse.mybir as mybir
from concourse.bass_utils import run_bass_kernel_spmd

F32 = mybir.dt.float32
BF16 = mybir.dt.bfloat16
AF = mybir.ActivationFunctionType
ALU = mybir.AluOpType
AX = mybir.AxisListType

CFG = {"SEQ": 4096, "NCORES": 8, "NS": 16, "DEPTH": 2, "DEBUG": None}

D = 1024
KC = 8
TT = 512
DIN = 6656
DFF = 4096
PLE = 256
EPS = 1e-6
DILS = (1, 4, 16)
KEEP = (128, 128, 512, 2048)
NSLOT = 2


def _es(dt):
    return 2 if dt == BF16 else 4


class K:
    def __init__(self, nc):
        self.nc = nc
        self.eng = {"pe": nc.tensor, "act": nc.scalar, "dve": nc.vector, "pool": nc.gpsimd, "sp": nc.sync}
        self.sem = {e: nc.semaphore("prog_" + e).__enter__() for e in self.eng}
        self.cnt = {e: 0 for e in self.eng}
        self.seen = {e: {} for e in self.eng}
        self.recs = {}
        self.dsem = {}
        self.dtot = {}
        self.notrack = set()
        self.ps = [nc.psum_tensor("psb%d" % i, [128, 512], F32).__enter__() for i in range(8)]
        self.psi = 0
        self.flip = 0
        self.nwait = 0
        self.dsem_c = {}
        self.batch = None

    def box(self, ap):
        es = _es(ap.dtype)
        steps = ap.ap
        if "DRAM" in str(ap.space).upper():
            lo = ap.offset
            hi = lo + sum(st * (c - 1) for st, c in steps)
            return (ap.name, 0, 1, lo * es, (hi + 1) * es)
        pstep, pcnt = steps[0]
        p0 = ap.offset // pstep if pstep > 0 else 0
        f0 = ap.offset - p0 * pstep
        hi = f0 + sum(st * (c - 1) for st, c in steps[1:])
        if ap.name.startswith("psb"):
            return (ap.name, 0, 128, 0, 1 << 30)
        return (ap.name, p0, p0 + pcnt, f0 * es, (hi + 1) * es)

    def _collect(self, ap, is_write, need, skip_key=None, eng=None):
        name, p0, p1, b0, b1 = self.box(ap)
        if name in self.notrack:
            return
        psum = name.startswith("psb")
        for r in self.recs.get(name, ()):
            if r[0] < p1 and p0 < r[1] and r[2] < b1 and b0 < r[3]:
                if (not is_write) and r[4] == "r" and (not psum or r[5] == eng):
                    continue
                if skip_key is not None and r[5] == skip_key:
                    continue
                v = r[6] if isinstance(r[6], int) else r[6][0]
                if v is None:
                    continue
                if need.get(r[5], 0) < v:
                    need[r[5]] = v

    def _record(self, ap, kind, key, val):
        name, p0, p1, b0, b1 = self.box(ap)
        if name in self.notrack:
            return
        lst = self.recs.setdefault(name, [])
        if kind == "w":
            lst[:] = [r for r in lst if not (p0 <= r[0] and r[1] <= p1 and b0 <= r[2] and r[3] <= b1)]
        else:
            lst[:] = [r for r in lst if not (r[4] == "r" and r[5] == key and p0 <= r[0] and r[1] <= p1
                                             and b0 <= r[2] and r[3] <= b1)]
        lst.append((p0, p1, b0, b1, kind, key, val))

    def _semof(self, key):
        return self.sem[key] if key in self.sem else self.dsem[key]

    def _wait(self, e, need):
        for key, val in need.items():
            if self.seen[e].get(key, 0) < val:
                self.eng[e].wait_ge(self._semof(key), val)
                self.seen[e][key] = val
                self.nwait += 1

    def issue(self, e, fn, reads=(), writes=(), force=False):
        need = {}
        sk = "pe" if (e == "pe" and not force) else None
        for ap in reads:
            self._collect(ap, False, need, sk, e)
        for ap in writes:
            self._collect(ap, True, need, sk, e)
        self._wait(e, need)
        ins = fn()
        self.cnt[e] += 1
        ins.then_inc(self.sem[e], 1)
        c = self.cnt[e]
        for ap in reads:
            self._record(ap, "r", e, c)
        for ap in writes:
            self._record(ap, "w", e, c)
        return ins

    NDSEM = 96

    def _next_dsem(self, q="sp"):
        pre, n = ("sq", 12) if q == "pool" else ("dq", 28)
        i = self.dsem_c.get(pre, 0) % n
        self.dsem_c[pre] = self.dsem_c.get(pre, 0) + 1
        key = "%s%d" % (pre, i)
        if key not in self.dsem:
            self.dsem[key] = self.nc.semaphore(key).__enter__()
            self.dtot[key] = 0
        return key

    def batch_begin(self, q="sp"):
        assert self.batch is None
        key = self._next_dsem(q)
        self.batch = {"key": key, "cell": [None], "first": True}

    def batch_end(self):
        self.batch["cell"][0] = self.dtot[self.batch["key"]]
        self.batch = None

    def dma(self, q, out, in_, tag=None, **kw):
        need = {}
        self._collect(in_, False, need)
        self._collect(out, True, need)
        if self.batch is None:
            key = self._next_dsem(q)
            first = True
        else:
            key = self.batch["key"]
            first = self.batch["first"]
            self.batch["first"] = False
        if first and self.dtot[key] > 0:
            need[key] = max(need.get(key, 0), self.dtot[key])
        self._wait(q, need)
        ins = self.eng[q].dma_start(out=out, in_=in_, **kw)
        ins.then_inc(self.dsem[key], 16)
        self.dtot[key] += 16
        val = self.dtot[key] if self.batch is None else self.batch["cell"]
        self._record(in_, "r", key, val)
        self._record(out, "w", key, val)

    def finish(self):
        need = {t: v for t, v in self.dtot.items() if v > 0}
        for e in ("pe", "act", "dve", "pool"):
            if self.cnt[e] > 0:
                need[e] = self.cnt[e]
        self._wait("sp", need)

    def psum(self):
        t = self.ps[self.psi]
        self.psi = (self.psi + 1) % 8
        return t

    def mm(self, out, lhsT, rhs, start=True, stop=True, force=False):
        return self.issue("pe", lambda: self.nc.tensor.matmul(out, lhsT, rhs, start=start, stop=stop),
                          reads=[lhsT, rhs], writes=[out], force=force)

    def tr(self, out, in_, ident):
        return self.issue("pe", lambda: self.nc.tensor.transpose(out, in_, ident), reads=[in_, ident], writes=[out])

    def act(self, out, in_, func, bias=None, scale=None):
        kw = {}
        rd = [in_]
        if bias is not None:
            kw["bias"] = bias
            if not isinstance(bias, (int, float)):
                rd.append(bias)
        if scale is not None:
            kw["scale"] = scale
            if not isinstance(scale, (int, float)):
                rd.append(scale)
        return self.issue("act", lambda: self.nc.scalar.activation(out=out, in_=in_, func=func, **kw),
                          reads=rd, writes=[out])

    def tt(self, out, in0, in1, op, eng="dve"):
        return self.issue(eng, lambda: self.eng[eng].tensor_tensor(out=out, in0=in0, in1=in1, op=op),
                          reads=[in0, in1], writes=[out])

    def ts(self, out, in0, s1, s2, op0, op1=None, eng="dve"):
        rd = [in0] + [s for s in (s1, s2) if s is not None and not isinstance(s, (int, float))]
        if op1 is None:
            f = lambda: self.eng[eng].tensor_scalar(out=out, in0=in0, scalar1=s1, scalar2=None, op0=op0)
        else:
            f = lambda: self.eng[eng].tensor_scalar(out=out, in0=in0, scalar1=s1, scalar2=s2, op0=op0, op1=op1)
        return self.issue(eng, f, reads=rd, writes=[out])

    def stt(self, out, in0, scalar, in1, op0, op1):
        rd = [in0, in1] + ([scalar] if not isinstance(scalar, (int, float)) else [])
        return self.issue("dve", lambda: self.nc.vector.scalar_tensor_tensor(
            out=out, in0=in0, scalar=scalar, in1=in1, op0=op0, op1=op1), reads=rd, writes=[out])

    def copy(self, out, in_, eng=None):
        if eng is None:
            self.flip ^= 1
            eng = "act" if self.flip else "dve"
        if eng == "act":
            return self.issue("act", lambda: self.nc.scalar.copy(out=out, in_=in_), reads=[in_], writes=[out])
        return self.issue(eng, lambda: self.eng[eng].tensor_copy(out=out, in_=in_), reads=[in_], writes=[out])

    def recip(self, out, in_):
        return self.issue("dve", lambda: self.nc.vector.reciprocal(out=out, in_=in_), reads=[in_], writes=[out])

    def memset(self, ap, v, eng="dve"):
        return self.issue(eng, lambda: self.eng[eng].memset(ap, v), reads=[], writes=[ap])

    def scan(self, out, d0, d1, init, op0=ALU.mult, op1=ALU.add):
        rd = [d0, d1] + ([init] if not isinstance(init, (int, float)) else [])
        return self.issue("dve", lambda: self.nc.vector.tensor_tensor_scan(
            out=out, data0=d0, data1=d1, initial=init, op0=op0, op1=op1), reads=rd, writes=[out])

    def reduce(self, out, in_, op, axis=AX.X):
        return self.issue("dve", lambda: self.nc.vector.tensor_reduce(out=out, in_=in_, axis=axis, op=op),
                          reads=[in_], writes=[out])


def host_consts():
    c = np.zeros((128, 1024), np.float32)
    kk = np.arange(128)[:, None]
    qq = np.arange(128)[None, :]
    c[:, 0:128] = np.eye(128, dtype=np.float32)
    c[:, 128:256] = (kk <= qq)
    c[:, 256:384] = (kk >= qq)
    c[:, 384:512] = ((qq // 16) >= (kk // 16))
    for i in range(8):
        for s in range(4):
            t = [t for t in range(i - 4, i) if t % 4 == s][0]
            k32 = np.arange(32)[:, None]
            q32 = np.arange(32)[None, :]
            ok = (t >= 0) & ((32 * t + k32) >= (32 * i + q32 - 128))
            c[32 * s:32 * s + 32, 512 + 32 * i:512 + 32 * i + 32] = ok
    k32 = np.arange(32)[:, None]
    q32 = np.arange(32)[None, :]
    for s in range(4):
        c[32 * s:32 * s + 32, 768:800] = (k32 <= q32)
    return c


def build(cfg):
    SEQ, NS, L = cfg["SEQ"], cfg["NS"], cfg["DEPTH"]
    NT = SEQ // TT
    NOSSM = cfg.get("NOSSM", False)
    NOSAMPLE = cfg.get("NOSAMPLE", False)
    keep = [min(w, SEQ) for w in KEEP]
    nc = bass.Bass("TRN2", target_bir_lowering=False, dynamic_dma_scratch_size=8192)
    k = K(nc)

    def din(name, shape):
        k.notrack.add(name)
        return nc.dram_tensor(name, list(shape), F32, kind="ExternalInput").ap()

    def dout(name, shape):
        k.notrack.add(name)
        return nc.dram_tensor(name, list(shape), F32, kind="ExternalOutput").ap()

    def sb(name, shape, dt):
        return nc.sbuf_tensor(name, list(shape), dt).__enter__()

    xT = din("xT", [D, SEQ])
    pT = din("pT", [L, PLE, SEQ])
    xsT = din("xsT", [D, NS])
    psT = din("psT", [L, PLE, NS])
    c_swa = din("c_swa", [L, NS, 128, 2, 2, 64])
    c_dil = [din("c_d1", [L, NS, 128, 2, 4, 64]), din("c_d4", [L, NS, 512, 2, 4, 64]),
             din("c_d16", [L, NS, 2048, 2, 4, 64])]
    st_ssm = din("st_ssm", [L, NS, 4096])
    w_in = din("w_in", [L, D, DIN])
    g_mix = din("g_mix", [L, D]); g_mlp = din("g_mlp", [L, D]); g_ple = din("g_ple", [L, D])
    g_final = din("g_final", [D])
    a_re = din("ssm_a_re", [L, 32, 64]); a_im = din("ssm_a_im", [L, 32, 64]); log_dt = din("ssm_log_dt", [L, 32])
    b_re = din("ssm_b_re", [L, 32, 64, 16]); b_im = din("ssm_b_im", [L, 32, 64, 16])
    c_re = din("ssm_c_re", [L, 32, 16, 64]); c_im = din("ssm_c_im", [L, 32, 16, 64])
    ssm_d = din("ssm_d", [L, 512]); w_glu = din("w_glu", [L, 512, 512]); b_glu = din("b_glu", [L, 512])
    sinks = din("attn_sinks", [L, 8])
    w_ba = din("w_branch_a", [L, 512, D]); w_bb = din("w_branch_b", [L, 512, D]); w_bc = din("w_branch_c", [L, 256, D])
    w_out = din("w_out", [L, D, D]); w_up = din("w_up", [L, D, DFF]); w_down = din("w_down", [L, DFF, D])
    w_pg = din("w_ple_gate", [L, D, D]); w_pp = din("w_ple_proj", [L, PLE, D])
    cst_d = din("cst", [128, 1024])

    yT = dout("yT", [D, SEQ]); ysT = dout("ysT", [D, NS])
    o_kv = [dout("o_swa", [L, keep[0], 256]), dout("o_d1", [L, keep[1], 512]),
            dout("o_d4", [L, keep[2], 512]), dout("o_d16", [L, keep[3], 512])]
    o_ssm = dout("o_ssm", [L, 4096])
    s_kv = [dout("s_swa", [L, NS, 256]), dout("s_d1", [L, NS, 512]), dout("s_d4", [L, NS, 512]),
            dout("s_d16", [L, NS, 512])]
    s_ssm = dout("s_ssm", [L, NS, 4096])
    xmid = nc.dram_tensor("xmid", [D, SEQ], F32, kind="Internal").ap()
    dbg = dout("dbg", [128, 4096]) if cfg.get("DEBUG") else None

    xt = sb("xt", [128, KC, TT], F32)
    hT = sb("hT", [128, KC, TT], BF16)
    wslots = [sb("wslot%d" % i, [128, 4096], BF16) for i in range(NSLOT)]
    kS = sb("kS", [128, 1, 2, TT], BF16)
    kSd = sb("kSd", [128, 2, 2, TT], BF16)
    vS = sb("vS", [128, 8, 128], BF16)
    kD0 = sb("kD0", [128, 2, 2, TT], BF16)
    vD0 = sb("vD0", [128, 8, 256], BF16)
    kD1 = sb("kD1", [128, 2, 2, TT], BF16)
    vD1 = sb("vD1", [128, 2, 4, 256], BF16)
    kD2 = sb("kD2", [128, 2, SEQ], BF16)
    vD2 = sb("vD2", [128, 16, 256], BF16)
    kR = [kD0, kD1, kD2]
    kRn = [2, 2, 1]
    cstf = sb("cst_sb", [128, 256], F32)
    cb = sb("cstb", [128, 1024], BF16)
    ones_b = sb("ones_b", [128, 128], BF16)
    ones_f = sb("ones_f", [128, 128], F32)
    gv = sb("gv", [128, 7, 8], F32)
    bglu = sb("bglu", [128, L, 4], F32)
    dcol = sb("dcol", [128, L, 4], F32)
    esink = sb("esink", [128, L * 8], F32)
    xs = sb("xs", [128, KC, NS], F32)
    hs = sb("hs", [128, KC, NS], BF16)
    AW = cfg.get("AW", 12288)
    arena = sb("arena", [128, AW], F32)

    def av(off, shape, dt):
        nel = 1
        for s_ in shape[1:]:
            nel *= s_
        nb = nel * _es(dt)
        assert off % 4 == 0 and off + nb <= AW * 4, (off, nb, AW * 4)
        ap = arena[:, off // 4:(off + nb + 3) // 4]
        if dt != F32:
            ap = ap.bitcast(dt)
        if len(shape) == 3:
            ap = ap.rearrange("p (a b) -> p a b", a=shape[1])
        elif len(shape) == 4:
            ap = ap.rearrange("p (a b c) -> p a b c", a=shape[1], b=shape[2])
        if shape[0] < 128:
            ap = ap[0:shape[0]]
        return ap

    ident_f = cstf[:, 0:128]
    ident_b = cb[:, 0:128]
    mcur = cb[:, 128:256]
    mprev = cb[:, 256:384]
    mT0 = cstf[:, 128:256]
    rows = av(0, [128, 512], F32)[0:1]
    m16 = cb[:, 512:768]
    mcur16 = cb[:, 768:800]

    ncd = dict(allow_slow_non_contiguous=True)

    cst_tmp = av(2048, [128, 1024], F32)
    k.dma("sp", cst_tmp[:, :], cst_d[:, :])
    k.copy(cb[:, :], cst_tmp[:, :], eng="dve")
    k.copy(cstf[:, 0:128], cst_tmp[:, 0:128], eng="dve")
    k.copy(cstf[:, 128:256], cst_tmp[:, 384:512], eng="dve")
    k.memset(ones_b[:, :], 1.0)
    k.memset(ones_f[:, :], 1.0)
    k.batch_begin()
    for l in range(L):
        for j, g in enumerate((g_mix, g_mlp, g_ple)):
            k.dma("sp", gv[:, 3 * l + j, :], g[l].rearrange("(k p) -> p k", p=128), **ncd)
        k.dma("sp", bglu[:, l, :], b_glu[l].rearrange("(k p) -> p k", p=128), **ncd)
        k.dma("sp", dcol[:, l, :], ssm_d[l].rearrange("(k p) -> p k", p=128), **ncd)
        k.dma("sp", rows[0:1, 8 * l:8 * l + 8], sinks[l:l + 1, :])
    k.dma("sp", gv[:, 6, :], g_final.rearrange("(k p) -> p k", p=128), **ncd)
    k.batch_end()
    ps = k.psum()
    k.mm(ps[:, 0:8 * L], ones_f[0:1, :], rows[0:1, 0:8 * L])
    k.act(esink[:, :], ps[:, 0:8 * L], AF.Exp)

    slot_i = [0]

    def wload(W, r0, nrows, c0, ncols):
        kc_ = nrows // 128
        s_ = wslots[slot_i[0] % NSLOT]
        slot_i[0] += 1
        view = s_[:, 0:kc_ * ncols].rearrange("p (k n) -> p k n", k=kc_)
        src = W[r0:r0 + nrows, c0:c0 + ncols].rearrange("(k p) n -> p k n", p=128)
        k.dma("pool", view, src)
        return view

    def rmsnorm(x3, gi, out3, N, tmp3):
        for kk in range(KC):
            k.act(tmp3[:, kk, :], x3[:, kk, :], AF.Square)
        ps = k.psum()
        for kk in range(KC):
            k.mm(ps[:, 0:N], ones_b[:, :], tmp3[:, kk, :], start=kk == 0, stop=kk == KC - 1)
        rs = av(RS_OFF, [128, TT], F32)[:, 0:N]
        k.act(rs, ps[:, 0:N], AF.Sqrt, bias=EPS, scale=1.0 / D)
        k.recip(rs, rs)
        for kk in range(KC):
            k.stt(out3[:, kk, :], x3[:, kk, :], gv[:, gi, kk:kk + 1], rs, ALU.mult, ALU.mult)

    RS_OFF = 0
    Q_S = 2048
    Q_D = Q_S + 4096
    O_A = Q_D + 6144
    O_B = O_A + 4096
    O_C = O_B + 4096
    PH = O_C + 2048
    PT_P = [PH, PH + 1024]
    PT_C = [PH + 2048, PH + 3072]
    RD = PH + 4096
    ACCN = RD + 2048
    ACCD = ACCN + 4096
    VC16 = ACCD + 8192
    OST = VC16 + 3072
    ATT_END = OST + 4096
    MRG = PH
    SA = MRG + 8192
    ACC = SA + 2048
    HID = Q_S
    YST = Q_S
    SG = Q_S + 32768

    qS = av(Q_S, [128, 4, TT], BF16)
    qD = av(Q_D, [128, 6, TT], BF16)
    oA = av(O_A, [128, 4, TT], BF16)
    oB = av(O_B, [128, 4, TT], BF16)
    oC = av(O_C, [128, 2, TT], BF16)
    ptp = [av(o, [128, TT], BF16) for o in PT_P]
    ptc = [av(o, [128, TT], BF16) for o in PT_C]
    rd = av(RD, [128, TT], F32)
    accN = av(ACCN, [128, 2, TT], F32)
    accD = av(ACCD, [128, 4, TT], F32)
    vc16 = av(VC16, [128, 6, 256], BF16)
    ost = [av(OST, [128, 512], F32), av(OST + 2048, [128, 512], F32)]
    ost_i = [0]

    def stage():
        ost_i[0] ^= 1
        return ost[ost_i[0]]

    def bc_heads(m, nh, nq):
        return m.unsqueeze(1).to_broadcast([m.shape[0], nh, nq])

    blk_i = [0]

    def qk_phase(nh, nq, qf, prev_segs, cur_k, cp0, pmask, cmask, split=False):
        b_ = blk_i[0] % 2
        blk_i[0] += 1
        st = {"nh": nh, "nq": nq, "cp0": cp0, "prev": bool(prev_segs)}
        npar = 2 if split else 1

        def bank_col(h):
            return (h % 2, (h // 2) * nq) if split else (0, h * nq)

        def exp_out(PT, rows, pss):
            v3 = PT[rows, 0:nh * nq].rearrange("p (h q) -> p h q", h=nh)
            if split:
                for par in range(2):
                    k.act(v3[:, par:nh:2, :], pss[par][rows, 0:(nh // 2) * nq].rearrange("p (h q) -> p h q", h=nh // 2),
                          AF.Exp, scale=0.125)
            else:
                k.act(PT[rows, 0:nh * nq], pss[0][rows, 0:nh * nq], AF.Exp, scale=0.125)
            return v3
        if prev_segs:
            psP = [k.psum() for _ in range(npar)]
            npk = sum(s_[2] for s_ in prev_segs)
            st["npk"] = npk
            for h in range(nh):
                bk, c0 = bank_col(h)
                for (kf, p0, nk) in prev_segs:
                    k.mm(psP[bk][p0:p0 + nk, c0:c0 + nq], kf(h), qf(h))
            PTp = ptp[b_]
            v3 = exp_out(PTp, slice(0, npk), psP)
            k.tt(v3, v3, bc_heads(pmask[0:npk], nh, nq), ALU.mult)
            st["PTp"] = PTp
        psC = [k.psum() for _ in range(npar)]
        for h in range(nh):
            bk, c0 = bank_col(h)
            k.mm(psC[bk][cp0:cp0 + nq, c0:c0 + nq], cur_k(h), qf(h))
        PTc = ptc[b_]
        v3 = exp_out(PTc, slice(cp0, cp0 + nq), psC)
        k.tt(v3, v3, bc_heads(cmask, nh, nq), ALU.mult)
        st["PTc"] = PTc
        return st

    def pv_phase(st, prev_v, cur_v, finish):
        nh, nq, cp0 = st["nh"], st["nq"], st["cp0"]
        psN = k.psum()
        psD = k.psum()
        for h in range(nh):
            out = psN[64 * (h % 2):64 * (h % 2) + 64, (h // 2) * nq:(h // 2 + 1) * nq]
            if st["prev"]:
                k.mm(out, prev_v(h), st["PTp"][0:st["npk"], h * nq:(h + 1) * nq], start=True, stop=False)
            k.mm(out, cur_v(h), st["PTc"][cp0:cp0 + nq, h * nq:(h + 1) * nq], start=not st["prev"], stop=True,
                 force=(st["prev"] and st["npk"] < 128))
        if st["prev"]:
            k.mm(psD[:, 0:nh * nq], ones_b[0:st["npk"], :], st["PTp"][0:st["npk"], 0:nh * nq], start=True, stop=False)
        k.mm(psD[:, 0:nh * nq], ones_b[cp0:cp0 + nq, :], st["PTc"][cp0:cp0 + nq, 0:nh * nq],
             start=not st["prev"], stop=True, force=(st["prev"] and st["npk"] < 128))
        finish(psN, psD)

    def run_pipelined(blocks):
        prev = None
        for qa, pa in blocks:
            st = qk_phase(*qa)
            if prev is not None:
                pv_phase(prev[0], *prev[1])
            prev = (st, pa)
        if prev is not None:
            pv_phase(prev[0], *prev[1])

    def attention(l, i):
        sl = i % 2
        blocks = []
        hpf0 = lambda h: slice(64 * (h % 2), 64 * (h % 2) + 64)
        for b in range(4):
            for hg in range(2):
                qf = lambda h, b=b, hg=hg: qS[hpf0(h), 2 * hg + h // 2, 128 * b:128 * b + 128]
                if b > 0:
                    segs = [(lambda h, b=b, hg=hg: kSd[hpf0(h), hg, sl, 128 * (b - 1):128 * b], 0, 128)]
                    pv_ = lambda h, b=b, hg=hg: vS[:, 4 * sl + b - 1, 64 * hg:64 * hg + 64]
                elif i > 0:
                    segs = [(lambda h, hg=hg: kSd[hpf0(h), hg, 1 - sl, 384:512], 0, 128)]
                    pv_ = lambda h, hg=hg: vS[:, 4 * (1 - sl) + 3, 64 * hg:64 * hg + 64]
                else:
                    segs = []
                    pv_ = None
                ck = lambda h, b=b, hg=hg: kSd[hpf0(h), hg, sl, 128 * b:128 * b + 128]
                cv = lambda h, b=b, hg=hg: vS[:, 4 * sl + b, 64 * hg:64 * hg + 64]

                def fin(psN, psD, b=b, hg=hg):
                    for h in range(4):
                        H = 4 * hg + h
                        k.ts(rd[:, 128 * h:128 * h + 128], psD[:, 128 * h:128 * h + 128],
                             esink[:, 8 * l + H:8 * l + H + 1], None, ALU.add)
                    k.recip(rd[:, :], rd[:, :])
                    for h in range(4):
                        H = 4 * hg + h
                        pp = slice(64 * (h % 2), 64 * (h % 2) + 64)
                        k.tt(oB[pp, H // 2, 128 * b:128 * b + 128], psN[pp, (h // 2) * 128:(h // 2) * 128 + 128],
                             rd[pp, 128 * h:128 * h + 128], ALU.mult)
                blocks.append(((4, 128, qf, segs, ck, 0, mprev, mcur, True), (pv_, cv, fin)))
        for gi, d in enumerate(DILS):
            kr = kR[gi]
            nsl = kRn[gi]
            s_c = i % nsl if gi < 2 else 0
            if d == 1:
                subs = [(0, b) for b in range(4)]
            elif d == 4:
                subs = [(r, 0) for r in range(4)]
            else:
                subs = [(r, 0) for r in range(16)]
            for (r, b) in subs:
                nq = 128 if d < 16 else 32
                cp0 = 0 if d < 16 else 32 * (r % 3)
                t0 = 128 * b + r
                tsl = slice(t0, t0 + d * (nq - 1) + 1, d)
                hpf = lambda h: slice(64 * (h % 2), 64 * (h % 2) + 64)
                qf = lambda h, gi=gi, tsl=tsl: qD[hpf(h), 2 * gi + h // 2, tsl]
                if gi < 2:
                    ck = lambda h, kr=kr, s_c=s_c, tsl=tsl: kr[hpf(h), h // 2, s_c, tsl]
                else:
                    ck = lambda h, r=r: kD2[hpf(h), h // 2, TT * i + r:TT * i + TT:16]
                if d == 1:
                    cv = lambda h, b=b: vD0[:, 4 * sl + b, 64 * h:64 * h + 64]
                    if b > 0:
                        segs = [(lambda h, kr=kr, s_c=s_c, b=b: kr[hpf(h), h // 2, s_c, 128 * (b - 1):128 * b], 0, 128)]
                        pv_ = lambda h, b=b: vD0[:, 4 * sl + b - 1, 64 * h:64 * h + 64]
                    elif i > 0:
                        segs = [(lambda h, kr=kr, s_c=s_c: kr[hpf(h), h // 2, 1 - s_c, 384:512], 0, 128)]
                        pv_ = lambda h: vD0[:, 4 * (1 - sl) + 3, 64 * h:64 * h + 64]
                    else:
                        segs, pv_ = [], None
                    pm, cm = mprev, mcur
                elif d == 4:
                    cv = lambda h, r=r: vD1[:, sl, r, 64 * h:64 * h + 64]
                    if i > 0:
                        segs = [(lambda h, kr=kr, s_c=s_c, tsl=tsl: kr[hpf(h), h // 2, 1 - s_c, tsl], 0, 128)]
                        pv_ = lambda h, r=r: vD1[:, 1 - sl, r, 64 * h:64 * h + 64]
                    else:
                        segs, pv_ = [], None
                    pm, cm = mprev, mcur
                else:
                    cv = lambda h, r=r, cp0=cp0: vc16[cp0:cp0 + 32, r // 3, 64 * h:64 * h + 64]
                    nw = min(i, 4)
                    if nw > 0:
                        t0_ = TT * (i - nw)
                        segs = [(lambda h, r=r, t0_=t0_: kD2[hpf(h), h // 2, t0_ + r:TT * i:16], 0, 32 * nw)]
                        pv_ = lambda h, r=r, nw=nw: vD2[0:32 * nw, r, 64 * h:64 * h + 64]
                    else:
                        segs, pv_ = [], None
                    pm = mprev[:, 0:32] if i >= 4 else ones_b[:, 0:32]
                    cm = mcur16[cp0:cp0 + 32, :]

                def fin(psN, psD, gi=gi, tsl=tsl, nq=nq):
                    n3 = psN[:, 0:2 * nq].rearrange("p (c q) -> p c q", c=2)
                    d3 = psD[:, 0:4 * nq].rearrange("p (c q) -> p c q", c=4)
                    if gi == 0:
                        k.copy(accN[:, :, tsl], n3)
                        k.copy(accD[:, :, tsl], d3)
                    else:
                        k.tt(accN[:, :, tsl], n3, accN[:, :, tsl], ALU.add)
                        k.tt(accD[:, :, tsl], d3, accD[:, :, tsl], ALU.add)
                blocks.append(((4, nq, qf, segs, ck, cp0, pm, cm, True), (pv_, cv, fin)))
        ATT = cfg.get("ATT", 15)
        blocks = [b_ for b_, g_ in zip(blocks, [0] * 8 + [1] * 4 + [2] * 4 + [3] * 16) if (ATT >> g_) & 1]
        run_pipelined(blocks)
        k.recip(accD[:, :, :], accD[:, :, :])
        for h in range(4):
            pp = slice(64 * (h % 2), 64 * (h % 2) + 64)
            k.tt(oC[pp, h // 2, :], accN[pp, h // 2, :], accD[pp, h, :], ALU.mult)
        if i >= 4:
            for a in range(3):
                k.dma("sp", vD2[32 * a:32 * a + 32, :, :], vD2[32 * a + 32:32 * a + 64, :, :])
        q4 = min(i, 3)
        for rq in range(3):
            n_ = len(range(rq, 16, 3))
            k.dma("sp", vD2[32 * q4:32 * q4 + 32, rq:16:3, :], vc16[32 * rq:32 * rq + 32, 0:n_, :])

    def win_phase(l, i):
        W = w_in[l]
        last = (i == NT - 1)
        sl = i % 2
        hk = lambda kk: hT[:, kk, :]

        def fm(wv, c0, evac, lhs=None):
            ps = k.psum()
            for kk in range(KC):
                lt = lhs(kk) if lhs is not None else wv[:, kk, c0:c0 + 128]
                k.mm(ps[:, :], lt, hk(kk), start=kk == 0, stop=kk == KC - 1)
            evac(ps)

        def tok(wv, c0, ncols, lhs_cols, M, evac):
            ps = k.psum()
            for kk in range(KC):
                k.mm(ps[0:M, 0:ncols], hT[:, kk, lhs_cols], wv[:, kk, c0:c0 + ncols], start=kk == 0, stop=kk == KC - 1)
            evac(ps)

        w1 = wload(W, 0, D, 512, 512)
        for c in range(4):
            fm(w1, 128 * c, lambda ps, c=c: k.copy(qS[:, c, :], ps[:, :]))
        w2 = wload(W, 0, D, 1024, 512)
        def ev_ks(ps):
            k.copy(kS[:, 0, sl, :], ps[:, :])
            k.copy(kSd[0:64, 0, sl, :], ps[0:64, :])
            k.copy(kSd[64:128, 1, sl, :], ps[64:128, :])
            k.batch_begin()
            k.dma("sp", kSd[64:128, 0, sl, :], kS[0:64, 0, sl, :])
            k.dma("sp", kSd[0:64, 1, sl, :], kS[64:128, 0, sl, :])
            k.batch_end()
        fm(w2, 0, ev_ks)
        for j in range(4):
            if last and j == 3:
                def ev(ps, j=j):
                    st_ = stage()
                    k.copy(st_[:, 0:256], ps[:, 0:256])
                    k.copy(vS[:, 4 * sl + j, :], ps[:, 128:256])
                    k.dma("sp", o_kv[0][l, :, :], st_[:, 0:256])
                tok(w2, 0, 256, slice(128 * j, 128 * j + 128), 128, ev)
            else:
                tok(w2, 128, 128, slice(128 * j, 128 * j + 128), 128,
                    lambda ps, j=j: k.copy(vS[:, 4 * sl + j, :], ps[:, 0:128]))
        for c in range(2):
            fm(w2, 256 + 128 * c, lambda ps, c=c: k.copy(qD[:, c, :], ps[:, :]))
        w3 = wload(W, 0, D, 1536, 512)
        for c in range(2):
            fm(w3, 128 * c, lambda ps, c=c: k.copy(kD0[:, c, sl, :], ps[:, :]))
        for j in range(4):
            if last and j == 3:
                def ev(ps, j=j):
                    st_ = stage()
                    k.copy(st_[:, :], ps[:, :])
                    k.copy(vD0[:, 4 * sl + j, :], ps[:, 256:512])
                    k.dma("sp", o_kv[1][l, :, :], st_[:, :])
                tok(w3, 0, 512, slice(128 * j, 128 * j + 128), 128, ev)
            else:
                tok(w3, 256, 256, slice(128 * j, 128 * j + 128), 128,
                    lambda ps, j=j: k.copy(vD0[:, 4 * sl + j, :], ps[:, 0:256]))
        w4 = wload(W, 0, D, 2048, 512)
        for c in range(2):
            fm(w4, 128 * c, lambda ps, c=c: k.copy(qD[:, 2 + c, :], ps[:, :]))
        for c in range(2):
            fm(w4, 256 + 128 * c, lambda ps, c=c: k.copy(kD1[:, c, sl, :], ps[:, :]))
        if last:
            for r in range(4):
                def ev(ps, r=r):
                    st_ = stage()
                    k.copy(st_[:, 0:256], ps[:, 0:256])
                    k.dma("sp", o_kv[2][l, r:keep[2]:4, 0:256], st_[:, 0:256])
                tok(w4, 256, 256, slice(r, 512, 4), 128, ev)
        w5 = wload(W, 0, D, 2560, 512)
        for r in range(4):
            def ev(ps, r=r):
                k.copy(vD1[:, sl, r, :], ps[:, 0:256])
                if last:
                    st_ = stage()
                    k.copy(st_[:, 0:256], ps[:, 0:256])
                    k.dma("sp", o_kv[2][l, r:keep[2]:4, 256:512], st_[:, 0:256])
            tok(w5, 0, 256, slice(r, 512, 4), 128, ev)
        for c in range(2):
            fm(w5, 256 + 128 * c, lambda ps, c=c: k.copy(qD[:, 4 + c, :], ps[:, :]))
        w6 = wload(W, 0, D, 3072, 512)
        for c in range(2):
            fm(w6, 128 * c, lambda ps, c=c: k.copy(kD2[:, c, TT * i:TT * i + TT], ps[:, :]))
        for r in range(16):
            p0 = 32 * (r % 3)
            ps = k.psum()
            for kk in range(KC):
                k.mm(ps[p0:p0 + 32, 0:256], hT[:, kk, r:512:16], w6[:, kk, 256:512], start=kk == 0, stop=kk == KC - 1)
            k.copy(vc16[p0:p0 + 32, r // 3, :], ps[p0:p0 + 32, 0:256])
        ntail = keep[3] // TT
        if i >= NT - ntail:
            row0 = (i - (NT - ntail)) * TT
            for j in range(4):
                def ev(ps, j=j):
                    st_ = stage()
                    k.copy(st_[:, :], ps[:, :])
                    k.dma("sp", o_kv[3][l, row0 + 128 * j:row0 + 128 * j + 128, :], st_[:, :])
                tok(w6, 0, 512, slice(128 * j, 128 * j + 128), 128, ev)

    def wload_pack(items):
        s_ = wslots[slot_i[0] % NSLOT]
        slot_i[0] += 1
        off = 0
        views = []
        for (W, r0, nrows, c0, ncols) in items:
            kc_ = nrows // 128
            view = s_[:, off:off + kc_ * ncols].rearrange("p (k n) -> p k n", k=kc_)
            off += kc_ * ncols
            assert off <= 4096
            k.dma("pool", view, W[r0:r0 + nrows, c0:c0 + ncols].rearrange("(k p) n -> p k n", p=128))
            views.append(view)
        return views

    def merge_wout(l, hT_, oA_, oB_, oC_, x_, N, merged, sa, acc):
        W = w_in[l]
        wbr = [w_ba[l], w_bb[l], w_bc[l]]
        nkb = [4, 4, 2]
        ob = [oA_, oB_, oC_]
        for mq in range(4):
            for b_ in range(3):
                wg, wb = wload_pack([(W, 0, D, 3584 + 1024 * b_ + 256 * mq, 256),
                                     (wbr[b_], 0, 128 * nkb[b_], 256 * mq, 256)])
                for mm_ in range(2):
                    m = 2 * mq + mm_
                    ps = k.psum()
                    for kk in range(KC):
                        k.mm(ps[:, 0:N], wg[:, kk, 128 * mm_:128 * mm_ + 128], hT_[:, kk, 0:N],
                             start=kk == 0, stop=kk == KC - 1)
                    k.act(sa[:, 0:N], ps[:, 0:N], AF.Sigmoid)
                    ps2 = k.psum()
                    for kk in range(nkb[b_]):
                        k.mm(ps2[:, 0:N], wb[:, kk, 128 * mm_:128 * mm_ + 128], ob[b_][:, kk, 0:N],
                             start=kk == 0, stop=kk == nkb[b_] - 1)
                    if b_ == 0:
                        k.tt(acc[:, mm_, 0:N], ps2[:, 0:N], sa[:, 0:N], ALU.mult)
                    else:
                        k.tt(sa[:, 0:N], ps2[:, 0:N], sa[:, 0:N], ALU.mult)
                        if b_ == 1:
                            k.tt(acc[:, mm_, 0:N], acc[:, mm_, 0:N], sa[:, 0:N], ALU.add)
                        else:
                            k.tt(merged[:, m, 0:N], acc[:, mm_, 0:N], sa[:, 0:N], ALU.add)
        for mg in range(2):
            wo = wload(w_out[l], 0, D, 512 * mg, 512)
            for mm_ in range(4):
                m = 4 * mg + mm_
                ps = k.psum()
                for kk in range(KC):
                    k.mm(ps[:, 0:N], wo[:, kk, 128 * mm_:128 * mm_ + 128], merged[:, kk, 0:N],
                         start=kk == 0, stop=kk == KC - 1)
                k.tt(x_[:, m, 0:N], ps[:, 0:N], x_[:, m, 0:N], ALU.add)

    def mlp(l, hT_, x_, N, hid, tmp):
        for ug in range(8):
            wu = wload(w_up[l], 0, D, 512 * ug, 512)
            for mm_ in range(4):
                j = 4 * ug + mm_
                ps = k.psum()
                for kk in range(KC):
                    k.mm(ps[:, 0:N], wu[:, kk, 128 * mm_:128 * mm_ + 128], hT_[:, kk, 0:N],
                         start=kk == 0, stop=kk == KC - 1)
                k.act(tmp[:, 0:N], ps[:, 0:N], AF.Relu)
                k.tt(hid[:, j, 0:N], tmp[:, 0:N], tmp[:, 0:N], ALU.mult)
        for mg in range(2):
            pss = [k.psum() for _ in range(4)]
            for ku in range(4):
                wd = wload(w_down[l], 1024 * ku, 1024, 512 * mg, 512)
                for mm_ in range(4):
                    for kk in range(KC):
                        k.mm(pss[mm_][:, 0:N], wd[:, kk, 128 * mm_:128 * mm_ + 128], hid[:, 8 * ku + kk, 0:N],
                             start=(ku == 0 and kk == 0), stop=(ku == 3 and kk == KC - 1))
            for mm_ in range(4):
                m = 4 * mg + mm_
                k.tt(x_[:, m, 0:N], pss[mm_][:, 0:N], x_[:, m, 0:N], ALU.add)

    def ple(l, hT_, x_, N, psrc, pb, sg, tmp):
        k.dma("pool", pb[:, :, 0:N], psrc.rearrange("(k p) t -> p k t", p=128))
        for mg in range(2):
            wg = wload(w_pg[l], 0, D, 512 * mg, 512)
            wp = wload(w_pp[l], 0, PLE, 512 * mg, 512)
            for mm_ in range(4):
                m = 4 * mg + mm_
                ps = k.psum()
                for kk in range(KC):
                    k.mm(ps[:, 0:N], wg[:, kk, 128 * mm_:128 * mm_ + 128], hT_[:, kk, 0:N],
                         start=kk == 0, stop=kk == KC - 1)
                k.act(sg[:, 0:N], ps[:, 0:N], AF.Sigmoid)
                ps2 = k.psum()
                for kk in range(2):
                    k.mm(ps2[:, 0:N], wp[:, kk, 128 * mm_:128 * mm_ + 128], pb[:, kk, 0:N], start=kk == 0, stop=kk == 1)
                k.tt(tmp[:, 0:N], ps2[:, 0:N], sg[:, 0:N], ALU.mult)
                k.tt(x_[:, m, 0:N], x_[:, m, 0:N], tmp[:, 0:N], ALU.add)

    merged = av(MRG, [128, 8, TT], BF16)
    sa = av(SA, [128, TT], F32)
    acc = av(ACC, [128, 2, TT], F32)
    hid = av(HID, [128, 32, TT], BF16)
    ytmp = av(YST, [128, 8, TT], F32)
    sg = av(SG, [128, TT], F32)
    tmpf = av(SG + 2048, [128, TT], F32)
    tmpb = av(SG + 4096, [128, TT], BF16)
    pbuf = av(Q_S, [128, 2, TT], BF16)

    Win = sb("Win", [128, 32, 128], BF16)
    Wo = sb("Wo", [128, 16, 2, 128], BF16)
    T0 = sb("T0w", [128, 32, 128], BF16)
    Ct = sb("Ct", [128, 16, 64], F32)
    St = sb("St", [128, 16, 64], F32)
    Rt = sb("Rt", [128, 16, 64], F32)
    SbB = sb("SbB", [128, 2, 16, 65], BF16)
    carry = sb("carry", [128, 16, 2], F32)
    lam1 = sb("lam1", [128, 2, 16], F32)
    Drep = sb("Drep", [128, 512], F32)
    SUB, MUL, ADD = ALU.subtract, ALU.mult, ALU.add

    def ssm_setup(l):
        off = [2048]

        def T(shape=(128, 16), dt=F32):
            shape = list(shape)
            nel = 1
            for s_ in shape[1:]:
                nel *= s_
            ap = av(off[0], shape, dt)
            off[0] += ((nel * _es(dt) + 3) // 4) * 4
            return ap
        Are, Aim = T(), T()
        k.batch_begin()
        for gg in range(2):
            hp = slice(64 * gg, 64 * gg + 64)
            k.dma("sp", Are[hp, :], a_re[l, gg:32:2, :].rearrange("j p -> p j"), **ncd)
            k.dma("sp", Aim[hp, :], a_im[l, gg:32:2, :].rearrange("j p -> p j"), **ncd)
        k.dma("sp", rows[0:1, 0:32], log_dt[l:l + 1, :])
        k.batch_end()
        ps = k.psum()
        k.mm(ps[:, 0:32], ones_f[0:1, :], rows[0:1, 0:32])
        dt_ = T()
        k.act(dt_[0:64, :], ps[0:64, 0:32:2], AF.Exp)
        k.act(dt_[64:128, :], ps[64:128, 1:32:2], AF.Exp)
        th, rr = T(), T()
        k.tt(th, Aim, dt_, MUL)
        k.tt(rr, Are, dt_, MUL)
        rho1, irho1 = T(), T()
        k.act(rho1, rr, AF.Exp)
        k.act(irho1, rr, AF.Exp, scale=-1.0)
        phi, p2, acc_, sn, cs, t1, t2 = T(), T(), T(), T(), T(), T(), T()
        k.ts(phi, th, 0.125, None, MUL)
        k.tt(p2, phi, phi, MUL)
        k.memset(acc_, 1.0)
        for n in range(9, 0, -1):
            k.tt(acc_, acc_, p2, MUL)
            k.ts(acc_, acc_, -1.0 / ((2 * n) * (2 * n + 1)), 1.0, MUL, ADD)
        k.tt(sn, acc_, phi, MUL)
        k.memset(acc_, 1.0)
        for n in range(9, 0, -1):
            k.tt(acc_, acc_, p2, MUL)
            k.ts(acc_, acc_, -1.0 / ((2 * n - 1) * (2 * n)), 1.0, MUL, ADD)
        k.copy(cs, acc_, "dve")
        for _ in range(3):
            k.tt(t1, cs, cs, MUL)
            k.tt(t2, sn, sn, MUL)
            k.tt(sn, sn, cs, MUL)
            k.ts(sn, sn, 2.0, None, MUL)
            k.tt(cs, t1, t2, SUB)
        lr, li, ir, ii = T(), T(), T(), T()
        k.tt(lr, rho1, cs, MUL)
        k.tt(li, rho1, sn, MUL)
        k.tt(ir, irho1, cs, MUL)
        k.tt(ii, irho1, sn, MUL)
        k.ts(ii, ii, -1.0, None, MUL)
        k.copy(lam1[:, 0, :], lr, "dve")
        k.copy(lam1[:, 1, :], li, "dve")
        den, lm1, zr, zi = T(), T(), T(), T()
        k.tt(den, Are, Are, MUL)
        k.tt(t1, Aim, Aim, MUL)
        k.tt(den, den, t1, ADD)
        k.recip(den, den)
        k.ts(lm1, lr, -1.0, None, ADD)
        k.tt(zr, lm1, Are, MUL)
        k.tt(t1, li, Aim, MUL)
        k.tt(zr, zr, t1, ADD)
        k.tt(zr, zr, den, MUL)
        k.tt(zi, li, Are, MUL)
        k.tt(t1, lm1, Aim, MUL)
        k.tt(zi, zi, t1, SUB)
        k.tt(zi, zi, den, MUL)
        S3 = (128, 16, 16)
        Bre, Bim, Cre, Cim, Bbr, Bbi, u1, u2 = [T(S3) for _ in range(8)]
        k.batch_begin()
        for gg in range(2):
            hp = slice(64 * gg, 64 * gg + 64)
            k.dma("sp", Bre[hp, :, :], b_re[l, gg:32:2].rearrange("j p h -> p j h"), **ncd)
            k.dma("sp", Bim[hp, :, :], b_im[l, gg:32:2].rearrange("j p h -> p j h"), **ncd)
            for j in range(16):
                k.dma("sp", Cre[hp, j, :], c_re[l, 2 * j + gg].rearrange("h p -> p h"), **ncd)
                k.dma("sp", Cim[hp, j, :], c_im[l, 2 * j + gg].rearrange("h p -> p h"), **ncd)
        k.batch_end()
        bz = lambda z: z.unsqueeze(2).to_broadcast([128, 16, 16])
        k.tt(u1, Bre, bz(zr), MUL)
        k.tt(u2, Bim, bz(zi), MUL)
        k.tt(Bbr, u1, u2, SUB)
        k.tt(u1, Bim, bz(zr), MUL)
        k.tt(u2, Bre, bz(zi), MUL)
        k.tt(Bbi, u1, u2, ADD)
        LPr, LPi = T((128, 9, 16)), T((128, 9, 16))
        LQr, LQi = T((128, 8, 16)), T((128, 8, 16))
        for (Pr_, Pi_, xr_, xi_, n_) in ((LPr, LPi, lr, li, 9), (LQr, LQi, ir, ii, 8)):
            k.memset(Pr_[:, 0, :], 1.0)
            k.memset(Pi_[:, 0, :], 0.0)
            for kk in range(1, n_):
                k.tt(t1, Pr_[:, kk - 1, :], xr_, MUL)
                k.tt(t2, Pi_[:, kk - 1, :], xi_, MUL)
                k.tt(Pr_[:, kk, :], t1, t2, SUB)
                k.tt(t1, Pr_[:, kk - 1, :], xi_, MUL)
                k.tt(t2, Pi_[:, kk - 1, :], xr_, MUL)
                k.tt(Pi_[:, kk, :], t1, t2, ADD)
        S4 = (128, 16, 128)
        WTr, WTi, WPr, WPi = T(S4), T(S4), T(S4), T(S4)
        bp = lambda P, kk: P[:, kk, :].unsqueeze(2).to_broadcast([128, 16, 16])
        for a in range(8):
            sl_ = slice(16 * a, 16 * a + 16)
            k.tt(u1, Bbr, bp(LPr, 7 - a), MUL)
            k.tt(u2, Bbi, bp(LPi, 7 - a), MUL)
            k.tt(WTr[:, :, sl_], u1, u2, SUB)
            k.tt(u1, Bbr, bp(LPi, 7 - a), MUL)
            k.tt(u2, Bbi, bp(LPr, 7 - a), MUL)
            k.tt(WTi[:, :, sl_], u1, u2, ADD)
            k.tt(u1, Cre, bp(LPr, a + 1), MUL)
            k.tt(u2, Cim, bp(LPi, a + 1), MUL)
            k.tt(Wo[:, :, 0, sl_], u1, u2, SUB)
            k.tt(u1, Cre, bp(LPi, a + 1), MUL)
            k.tt(u2, Cim, bp(LPr, a + 1), MUL)
            k.tt(u1, u1, u2, ADD)
            k.ts(Wo[:, :, 1, sl_], u1, -1.0, None, MUL)
            k.tt(u1, Cre, bp(LQr, 7 - a), MUL)
            k.tt(u2, Cim, bp(LQi, 7 - a), MUL)
            k.tt(WPr[:, :, sl_], u1, u2, SUB)
            k.tt(u1, Cre, bp(LQi, 7 - a), MUL)
            k.tt(u2, Cim, bp(LQr, 7 - a), MUL)
            k.tt(u1, u1, u2, ADD)
            k.ts(WPi[:, :, sl_], u1, -1.0, None, MUL)
        for j in range(16):
            for ri, WT in enumerate((WTr, WTi)):
                ps = k.psum()
                k.tr(ps[:, 0:128], WT[:, j, :], ident_f)
                k.copy(Win[:, 2 * j:2 * j + 2, 64 * ri:64 * ri + 64], ps[:, 0:128].rearrange("p (g q) -> p g q", g=2))
        for j in range(16):
            for gg in range(2):
                hp = slice(64 * gg, 64 * gg + 64)
                ps = k.psum()
                k.mm(ps[:, 0:128], WTr[hp, j, :], WPr[hp, j, :], start=True, stop=False)
                k.mm(ps[:, 0:128], WTi[hp, j, :], WPi[hp, j, :], start=False, stop=True)
                k.tt(T0[:, 2 * j + gg, :], ps[:, 0:128], mT0, MUL)
        E1r, E1i, r8i, r8, Pr, Pi = T(), T(), T(), T(), T(), T()
        k.act(r8i, rr, AF.Exp, scale=-8.0)
        k.act(r8, rr, AF.Exp, scale=8.0)
        k.tt(E1r, LPr[:, 8, :], r8i, MUL)
        k.tt(E1i, LPi[:, 8, :], r8i, MUL)
        k.copy(Rt[:, :, :], r8.unsqueeze(2).to_broadcast([128, 16, 64]), "dve")
        k.copy(Ct[:, :, 0], E1r, "dve")
        k.copy(St[:, :, 0], E1i, "dve")
        k.copy(Pr, E1r, "dve")
        k.copy(Pi, E1i, "dve")
        v1, v2 = WTr[:, :, 0:32], WTr[:, :, 32:64]
        m = 1
        while m < 64:
            bP = lambda X, m=m: X.unsqueeze(2).to_broadcast([128, 16, m])
            k.tt(v1[:, :, 0:m], Ct[:, :, 0:m], bP(Pr), MUL)
            k.tt(v2[:, :, 0:m], St[:, :, 0:m], bP(Pi), MUL)
            k.tt(Ct[:, :, m:2 * m], v1[:, :, 0:m], v2[:, :, 0:m], SUB)
            k.tt(v1[:, :, 0:m], Ct[:, :, 0:m], bP(Pi), MUL)
            k.tt(v2[:, :, 0:m], St[:, :, 0:m], bP(Pr), MUL)
            k.tt(St[:, :, m:2 * m], v1[:, :, 0:m], v2[:, :, 0:m], ADD)
            k.tt(t1, Pr, Pr, MUL)
            k.tt(t2, Pi, Pi, MUL)
            k.tt(Pi, Pr, Pi, MUL)
            k.ts(Pi, Pi, 2.0, None, MUL)
            k.tt(Pr, t1, t2, SUB)
            m *= 2
        k.dma("sp", rows[0:1, 0:512], ssm_d[l:l + 1, :])
        ps = k.psum()
        k.mm(ps[:, 0:512], ones_f[0:1, :], rows[0:1, 0:512])
        k.copy(Drep[:, :], ps[:, :], "dve")
        k.memset(carry[:, :, :], 0.0)
        assert off[0] <= AW * 4, off[0]

    U_ = av(2048, [128, 32, 64], BF16)
    XT = [av(6144 + 1024 * n, [128, 4, 64], F32) for n in range(6)]
    V1p = av(16384, [128, 32, 128], BF16)
    Gtok = av(16384, [128, 8, 512], BF16)
    Ytok = av(24576, [128, 32, 128], F32)
    tg = av(40960, [128, 1024], F32)
    gT = av(45056, [128, 4, TT], BF16)

    def win_u(l, i):
        w0 = wload(w_in[l], 0, D, 0, 512)
        for a in range(8):
            ps = k.psum()
            for kk in range(KC):
                k.mm(ps[0:64, :], hT[:, kk, a:512:8], w0[:, kk, :], start=kk == 0, stop=kk == KC - 1)
            k.copy(V1p[0:64, :, 16 * a:16 * a + 16], ps[0:64, :].rearrange("c (g h) -> c g h", g=32))

    def ssm_tile(l, i):
        for ri in range(2):
            k.copy(SbB[:, ri, :, 0], carry[:, :, ri], "dve")
        for gb in range(2):
            ps = k.psum()
            psb_ = ps[:, :].bitcast(BF16)
            for gl in range(16):
                g = 16 * gb + gl
                k.tr(psb_[:, 64 * gl:64 * gl + 64], V1p[0:64, g, :], ident_b[0:64, 0:64])
            k.copy(U_[:, 16 * gb:16 * gb + 16, :], psb_[:, :].rearrange("p (g c) -> p g c", g=16))
        for q in range(4):
            psr, psi_ = k.psum(), k.psum()
            for jj in range(4):
                j = 4 * q + jj
                for gg in range(2):
                    g = 2 * j + gg
                    hp = slice(64 * gg, 64 * gg + 64)
                    k.mm(psr[hp, 64 * jj:64 * jj + 64], Win[:, g, 0:64], U_[:, g, :])
                    k.mm(psi_[hp, 64 * jj:64 * jj + 64], Win[:, g, 64:128], U_[:, g, :])
            Xr = psr[:, 0:256].rearrange("p (j c) -> p j c", j=4)
            Xi = psi_[:, 0:256].rearrange("p (j c) -> p j c", j=4)
            Cq, Sq = Ct[:, 4 * q:4 * q + 4, :], St[:, 4 * q:4 * q + 4, :]
            a_, b_, xr, xi, wr, wi = XT
            k.tt(a_, Xr, Cq, MUL)
            k.tt(b_, Xi, Sq, MUL)
            k.tt(xr, a_, b_, ADD)
            k.tt(a_, Xi, Cq, MUL)
            k.tt(b_, Xr, Sq, MUL)
            k.tt(xi, a_, b_, SUB)
            for jj in range(4):
                j = 4 * q + jj
                k.scan(wr[:, jj, :], Rt[:, j, :], xr[:, jj, :], carry[:, j, 0:1])
                k.scan(wi[:, jj, :], Rt[:, j, :], xi[:, jj, :], carry[:, j, 1:2])
            k.tt(a_, wr, Cq, MUL)
            k.tt(b_, wi, Sq, MUL)
            k.tt(xr, a_, b_, SUB)
            k.tt(a_, wr, Sq, MUL)
            k.tt(b_, wi, Cq, MUL)
            k.tt(xi, a_, b_, ADD)
            k.copy(SbB[:, 0, 4 * q:4 * q + 4, 1:65], xr, "dve")
            k.copy(SbB[:, 1, 4 * q:4 * q + 4, 1:65], xi, "dve")
            k.copy(carry[:, 4 * q:4 * q + 4, 0], xr[:, :, 63], "dve")
            k.copy(carry[:, 4 * q:4 * q + 4, 1], xi[:, :, 63], "dve")
        Dv = Drep[0:64, :].rearrange("c (g h) -> c g h", g=32).unsqueeze(2).to_broadcast([64, 32, 8, 16])
        k.tt(Ytok[0:64, :, :].rearrange("c g (a h) -> c g a h", a=8),
             V1p[0:64, :, :].rearrange("c g (a h) -> c g a h", a=8), Dv, MUL)
        for q in range(4):
            pse, pso = k.psum(), k.psum()
            for jj in range(4):
                j = 4 * q + jj
                for gg, pb_ in ((0, pse), (1, pso)):
                    g = 2 * j + gg
                    hp = slice(64 * gg, 64 * gg + 64)
                    out = pb_[0:64, 128 * jj:128 * jj + 128]
                    k.mm(out, U_[:, g, :], T0[:, g, :], start=True, stop=False)
                    k.mm(out, SbB[hp, 0, j, 0:64], Wo[hp, j, 0, :], start=False, stop=False)
                    k.mm(out, SbB[hp, 1, j, 0:64], Wo[hp, j, 1, :], start=False, stop=True)
            for gg, pb_ in ((0, pse), (1, pso)):
                dst = Ytok[0:64, 8 * q + gg:8 * q + 8:2, :]
                k.tt(dst, pb_[0:64, 0:512].rearrange("c (j x) -> c j x", j=4), dst, ADD)
        for q in range(4):
            y = Ytok[0:64, 8 * q:8 * q + 8, :]
            t = tg[0:64, :].rearrange("c (g x) -> c g x", g=8)
            k.act(t, y, AF.Square)
            k.ts(t, t, 0.044715, 1.0, MUL, ADD)
            k.tt(t, t, y, MUL)
            k.act(t, t, AF.Sigmoid, scale=1.5957691216057308)
            out = Gtok[0:64, :, 128 * q:128 * q + 128].rearrange("c a (g h) -> c g a h", g=8)
            k.tt(out, y.rearrange("c g (a h) -> c g a h", a=8), t.rearrange("c g (a h) -> c g a h", a=8), MUL)
        for half in range(2):
            ps = k.psum()
            psb_ = ps[:, :].bitcast(BF16)
            for al in range(4):
                for fc in range(4):
                    o_ = 64 * (4 * al + fc)
                    k.tr(psb_[:, o_:o_ + 64], Gtok[0:64, 4 * half + al, 128 * fc:128 * fc + 128], ident_b[0:64, 0:64])
            for fc in range(4):
                out = gT[:, fc, :].rearrange("p (c a) -> p a c", a=8)[:, 4 * half:4 * half + 4, :]
                in_ = psb_[:, :].rearrange("p (a f c) -> p f a c", a=4, f=4)[:, fc]
                k.copy(out, in_)
        wgl = wload(w_glu[l], 0, 512, 0, 512)
        sa_ = tg[:, 0:512]
        for m in range(4):
            ps = k.psum()
            for kk in range(4):
                k.mm(ps[:, :], wgl[:, kk, 128 * m:128 * m + 128], gT[:, kk, :], start=kk == 0, stop=kk == 3)
            k.act(sa_, ps[:, :], AF.Sigmoid, bias=bglu[:, l, m:m + 1])
            k.tt(oA[:, m, :], gT[:, m, :], sa_, MUL)
        if i == NT - 1:
            for gg in range(2):
                k.dma("sp", o_ssm[l].rearrange("(j gg p ri) -> gg p j ri", j=16, gg=2, p=64)[gg],
                      carry[64 * gg:64 * gg + 64, :, :], **ncd)

    Us = sb("Us", [128, 32, NS], BF16)
    sinkp = sb("sinkp", [32, L, 4], F32)
    k.memset(Us[:, :, :], 0.0)
    k.batch_begin()
    for b in range(NS):
        for l_ in range(L):
            for kv in range(2):
                k.dma("sp", sinkp[NS * kv + b:NS * kv + b + 1, l_, :], sinks[l_:l_ + 1, 4 * kv:4 * kv + 4])
    k.batch_end()
    k.dma("sp", xs[:, :, :], xsT.rearrange("(k p) t -> p k t", p=128))

    def sample_layer(l):
        off = [2048]

        def T(shape, dt=F32):
            shape = list(shape)
            nel = 1
            for s_ in shape[1:]:
                nel *= s_
            ap = av(off[0], [128] + shape[1:], dt)
            off[0] += ((nel * _es(dt) + 3) // 4) * 4
            assert off[0] <= AW * 4, off[0]
            return ap[0:shape[0]] if shape[0] < 128 else ap
        ztok = T([NS, 3584])
        big0 = off[0]
        Hnat = T([NS, 4096])
        off[0] = big0
        KCH = 16
        kvb = [T([64, KCH, 64]), T([64, KCH, 64])]
        prod_off = off[0]
        prod = T([64, 32, 64])
        rmsnorm(xs, 3 * l + 0, hs, NS, hs)
        W = w_in[l]
        uTs = T([128, 4, NS], BF16)
        oAs, oBs, oCs = T([128, 4, NS], BF16), T([128, 4, NS], BF16), T([128, 2, NS], BF16)
        mark = off[0]
        for un in range(7):
            wv = wload(W, 0, D, 512 * un, 512)
            ps = k.psum()
            for kk in range(KC):
                k.mm(ps[0:NS, :], hs[:, kk, :], wv[:, kk, :], start=kk == 0, stop=kk == KC - 1)
            k.copy(ztok[:, 512 * un:512 * un + 512], ps[0:NS, :])
            if un == 0:
                for c in range(4):
                    ps2 = k.psum()
                    for kk in range(KC):
                        k.mm(ps2[:, 0:NS], wv[:, kk, 128 * c:128 * c + 128], hs[:, kk, :], start=kk == 0, stop=kk == KC - 1)
                    k.copy(uTs[:, c, :], ps2[:, 0:NS])
        k.dma("sp", s_kv[0][l, :, :], ztok[:, 1024:1280])
        for gi in range(3):
            c0 = 1280 + 768 * gi + 256
            k.dma("sp", s_kv[1 + gi][l, :, :], ztok[:, c0:c0 + 512])
        k.batch_begin()
        for g8 in range(8):
            k.dma("sp", Us[112:128, g8:32:8, :], uTs[16 * g8:16 * g8 + 16, 0:4, :])
        k.batch_end()
        k.dma("sp", Hnat[:, :], st_ssm[l, :, :])
        H0 = [T([128, 16, NS]), T([128, 16, NS])]
        H0b = [T([128, 16, NS], BF16), T([128, 16, NS], BF16)]
        for ri in range(2):
            ps = k.psum()
            for j in range(16):
                k.tr(ps[:, NS * j:NS * j + NS], Hnat[:, 256 * j + ri:256 * j + 256:2], ident_f[0:NS, 0:NS])
            k.copy(H0[ri][:, :, :], ps[:, 0:16 * NS].rearrange("p (j b) -> p j b", j=16), "dve")
            k.copy(H0b[ri][:, :, :], H0[ri][:, :, :], "dve")
        psx = [k.psum(), k.psum()]
        for j in range(16):
            for gg in range(2):
                g = 2 * j + gg
                hp = slice(64 * gg, 64 * gg + 64)
                for ri in range(2):
                    k.mm(psx[ri][hp, NS * j:NS * j + NS], Win[:, g, 64 * ri:64 * ri + 64], Us[:, g, :])
        sN = [T([128, 16, NS]), T([128, 16, NS])]
        ta, tb = T([128, 16, NS]), T([128, 16, NS])
        bl = lambda ri: lam1[:, ri, :].unsqueeze(2).to_broadcast([128, 16, NS])
        X3 = lambda ri: psx[ri][:, 0:16 * NS].rearrange("p (j b) -> p j b", j=16)
        k.tt(ta, H0[0], bl(0), MUL)
        k.tt(tb, H0[1], bl(1), MUL)
        k.tt(ta, ta, tb, SUB)
        k.tt(sN[0], X3(0), ta, ADD)
        k.tt(ta, H0[1], bl(0), MUL)
        k.tt(tb, H0[0], bl(1), MUL)
        k.tt(ta, ta, tb, ADD)
        k.tt(sN[1], X3(1), ta, ADD)
        Snat = Hnat
        for q in range(4):
            for ri in range(2):
                ps = k.psum()
                for jj in range(4):
                    j = 4 * q + jj
                    k.tr(ps[0:NS, 128 * jj:128 * jj + 128], sN[ri][:, j, :], ident_f)
                k.copy(Snat[:, 1024 * q + ri:1024 * q + 1024:2], ps[0:NS, :], "dve")
        k.dma("sp", s_ssm[l, :, :], Snat[:, :])
        psy = [k.psum(), k.psum()]
        for j in range(16):
            for gg in range(2):
                g = 2 * j + gg
                hp = slice(64 * gg, 64 * gg + 64)
                out = psy[gg][0:NS, 16 * j:16 * j + 16]
                k.mm(out, Us[:, g, :], T0[:, g, 112:128], start=True, stop=False)
                k.mm(out, H0b[0][hp, j, :], Wo[hp, j, 0, 0:16], start=False, stop=False)
                k.mm(out, H0b[1][hp, j, :], Wo[hp, j, 1, 0:16], start=False, stop=True)
        ys = T([NS, 512])
        tq = T([NS, 512])
        k.tt(ys, ztok[:, 0:512], Drep[0:NS, :], MUL)
        for gg in range(2):
            dst = ys.rearrange("b (j gg h) -> b j gg h", j=16, gg=2)[:, :, gg, :]
            k.tt(dst, psy[gg][0:NS, 0:256].rearrange("b (j h) -> b j h", j=16), dst, ADD)
        k.act(tq, ys, AF.Square)
        k.ts(tq, tq, 0.044715, 1.0, MUL, ADD)
        k.tt(tq, tq, ys, MUL)
        k.act(tq, tq, AF.Sigmoid, scale=1.5957691216057308)
        Gs = T([NS, 512], BF16)
        k.tt(Gs, ys, tq, MUL)
        gTs = T([128, 4, NS], BF16)
        ps = k.psum()
        psb_ = ps[:, :].bitcast(BF16)
        for fc in range(4):
            k.tr(psb_[:, NS * fc:NS * fc + NS], Gs[:, 128 * fc:128 * fc + 128], ident_b[0:NS, 0:NS])
        k.copy(gTs[:, :, :], psb_[:, 0:4 * NS].rearrange("p (f b) -> p f b", f=4))
        wgl = wload(w_glu[l], 0, 512, 0, 512)
        sas = T([128, NS])
        for m in range(4):
            ps = k.psum()
            for kk in range(4):
                k.mm(ps[:, 0:NS], wgl[:, kk, 128 * m:128 * m + 128], gTs[:, kk, :], start=kk == 0, stop=kk == 3)
            k.act(sas, ps[:, 0:NS], AF.Sigmoid, bias=bglu[:, l, m:m + 1])
            k.tt(oAs[:, m, :], gTs[:, m, :], sas, MUL)

        off[0] = mark
        def attn_core(nP, nqh, qsel, knew, vnew, cache, Lb, d, nh, sink_l):
            s = T([nP, nqh, 132])
            NCB = 128 // KCH
            for cb in range(NCB):
                Kc = kvb[cb % 2]
                k.batch_begin()
                for h in range(nh):
                    src = cache[l, :, (KCH * cb) * d:(KCH * cb + KCH) * d:d, 0, h, :]
                    k.dma("sp", Kc[NS * h:NS * h + NS, :, :], src)
                k.batch_end()
                for qh in range(nqh):
                    k.tt(prod[0:nP, 0:KCH], Kc[0:nP], qsel[:, qh, :].unsqueeze(1).to_broadcast([nP, KCH, 64]), MUL)
                    k.reduce(s[:, qh, KCH * cb:KCH * cb + KCH], prod[0:nP, 0:KCH], ALU.add)
            for qh in range(nqh):
                k.tt(prod[0:nP, 0, :], knew, qsel[:, qh, :], MUL)
                k.reduce(s[:, qh, 128:129], prod[0:nP, 0, :], ALU.add)
            m = T([nP, nqh])
            negm = T([nP, nqh])
            den = T([nP, nqh])
            k.reduce(m, s[:, :, 0:129], ALU.max)
            k.ts(negm, m, -0.125, None, MUL)
            p = T([nP, nqh, 132])
            for qh in range(nqh):
                k.act(p[:, qh, 0:129], s[:, qh, 0:129], AF.Exp, bias=negm[:, qh:qh + 1], scale=0.125)
            k.reduce(den, p[:, :, 0:129], ALU.add)
            if sink_l is not None:
                es = T([nP, nqh])
                for qh in range(nqh):
                    k.act(es[:, qh:qh + 1], sinkp[0:nP, sink_l, qh:qh + 1], AF.Exp, bias=negm[:, qh:qh + 1])
                k.tt(den, den, es, ADD)
            o = T([nP, nqh, 64])
            o2 = T([nP, 64])
            for qh in range(nqh):
                k.ts(o[:, qh, :], vnew, p[:, qh, 128:129], None, MUL)
            for cb in range(NCB):
                Vc = kvb[cb % 2]
                k.batch_begin()
                for h in range(nh):
                    src = cache[l, :, (KCH * cb) * d:(KCH * cb + KCH) * d:d, 1, h, :]
                    k.dma("sp", Vc[NS * h:NS * h + NS, :, :], src)
                k.batch_end()
                for qh in range(nqh):
                    k.tt(prod[0:nP, 0:KCH], Vc[0:nP],
                         p[:, qh, KCH * cb:KCH * cb + KCH].unsqueeze(2).to_broadcast([nP, KCH, 64]), MUL)
                    k.reduce(o2, prod[0:nP, 0:KCH].rearrange("p k d -> p d k"), ALU.add)
                    k.tt(o[:, qh, :], o[:, qh, :], o2, ADD)
            return o, m, den

        def to_bh(cols0, nP, nh, width=64):
            t_ = T([nP, width])
            k.batch_begin()
            for h in range(nh):
                k.dma("sp", t_[NS * h:NS * h + NS, :], ztok[:, cols0 + width * h:cols0 + width * h + width])
            k.batch_end()
            return t_
        nP = 2 * NS
        qs = T([nP, 4, 64])
        k.batch_begin()
        for kv in range(2):
            k.dma("sp", qs[NS * kv:NS * kv + NS, :, :],
                  ztok[:, 512 + 256 * kv:768 + 256 * kv].rearrange("b (q d) -> b q d", q=4))
        k.batch_end()
        kn = to_bh(1024, nP, 2)
        vn = to_bh(1152, nP, 2)
        o, m, den = attn_core(nP, 4, qs, kn, vn, c_swa, 128, 1, 2, l)
        k.recip(den, den)
        k.tt(o, o, den.unsqueeze(2).to_broadcast([nP, 4, 64]), MUL)
        otok = T([NS, 512])
        k.batch_begin()
        for kv in range(2):
            k.dma("sp", otok[:, 256 * kv:256 * kv + 256].rearrange("b (q d) -> b q d", q=4), o[NS * kv:NS * kv + NS, :, :])
        k.batch_end()
        otb = T([NS, 512], BF16)
        k.copy(otb, otok, "dve")
        ps = k.psum()
        psb_ = ps[:, :].bitcast(BF16)
        for fc in range(4):
            k.tr(psb_[:, NS * fc:NS * fc + NS], otb[:, 128 * fc:128 * fc + 128], ident_b[0:NS, 0:NS])
        k.copy(oBs[:, :, :], psb_[:, 0:4 * NS].rearrange("p (f b) -> p f b", f=4))
        nP = 4 * NS
        off[0] = mark
        res = []
        hold = [(T([nP, 1, 64]), T([nP, 1]), T([nP, 1])) for _ in range(3)]
        mark2 = off[0]
        for gi, d in enumerate(DILS):
            off[0] = mark2
            base = 1280 + 768 * gi
            qd = T([nP, 1, 64])
            k.batch_begin()
            for h in range(4):
                k.dma("sp", qd[NS * h:NS * h + NS, 0, :], ztok[:, base + 64 * h:base + 64 * h + 64])
            k.batch_end()
            kn = to_bh(base + 256, nP, 4)
            vn = to_bh(base + 512, nP, 4)
            o_, m_, d_ = attn_core(nP, 1, qd, kn, vn, c_dil[gi], 128 * d, d, 4, None)
            k.copy(hold[gi][0], o_, "dve")
            k.copy(hold[gi][1], m_, "dve")
            k.copy(hold[gi][2], d_, "dve")
            res.append(hold[gi])
        M = T([nP, 1])
        k.tt(M, res[0][1], res[1][1], ALU.max)
        k.tt(M, M, res[2][1], ALU.max)
        negM = T([nP, 1])
        k.ts(negM, M, -0.125, None, MUL)
        num = T([nP, 64])
        dsum = T([nP, 1])
        wg = T([nP, 1])
        tmpo = T([nP, 64])
        for gi in range(3):
            o, m, den = res[gi]
            k.act(wg, m, AF.Exp, bias=negM[:, 0:1], scale=0.125)
            if gi == 0:
                k.ts(num, o[:, 0, :], wg[:, 0:1], None, MUL)
                k.tt(dsum, den, wg, MUL)
            else:
                k.ts(tmpo, o[:, 0, :], wg[:, 0:1], None, MUL)
                k.tt(num, num, tmpo, ADD)
                k.tt(den, den, wg, MUL)
                k.tt(dsum, dsum, den, ADD)
        k.recip(dsum, dsum)
        k.ts(num, num, dsum[:, 0:1], None, MUL)
        otc = T([NS, 256])
        k.batch_begin()
        for h in range(4):
            k.dma("sp", otc[:, 64 * h:64 * h + 64], num[NS * h:NS * h + NS, :])
        k.batch_end()
        otcb = T([NS, 256], BF16)
        k.copy(otcb, otc, "dve")
        ps = k.psum()
        psb_ = ps[:, :].bitcast(BF16)
        for fc in range(2):
            k.tr(psb_[:, NS * fc:NS * fc + NS], otcb[:, 128 * fc:128 * fc + 128], ident_b[0:NS, 0:NS])
        k.copy(oCs[:, :, :], psb_[:, 0:2 * NS].rearrange("p (f b) -> p f b", f=2))
        mrg = T([128, 8, NS], BF16)
        sa_s = T([128, NS])
        acc_s = T([128, 2, NS])
        merge_wout(l, hs, oAs, oBs, oCs, xs, NS, mrg, sa_s, acc_s)
        rmsnorm(xs, 3 * l + 1, hs, NS, hs)
        hid_s = T([128, 32, NS], BF16)
        tb_s = T([128, NS], BF16)
        mlp(l, hs, xs, NS, hid_s, tb_s)
        rmsnorm(xs, 3 * l + 2, hs, NS, hs)
        pbs = T([128, 2, NS], BF16)
        sg_s = T([128, NS])
        tf_s = T([128, NS])
        ple(l, hs, xs, NS, psT[l], pbs, sg_s, tf_s)
        if l == L - 1:
            yts = T([128, KC, NS])
            rmsnorm(xs, 6, yts, NS, hs)
            k.dma("sp", ysT.rearrange("(k p) t -> p k t", p=128), yts[:, :, :])

    def ring_reset():
        for t_ in (kS, kSd, vS, kD0, vD0, kD1, vD1, kD2, vD2):
            ap = t_[:]
            k.memset(ap, 0.0)

    SSM = cfg.get("_ssm")

    STOP = cfg.get("STOP", 0)

    class _Stop(Exception):
        pass

    def stop_at(n):
        if STOP == n:
            raise _Stop()
    try:
        for l in range(L):
            stop_at(1)
            ring_reset()
            if not NOSSM:
                ssm_setup(l)
            for i in range(NT):
                tsl = slice(TT * i, TT * i + TT)
                src_x = xT if l == 0 else xmid
                k.dma("sp", xt[:, :, :], src_x[:, tsl].rearrange("(k p) t -> p k t", p=128))
                rmsnorm(xt, 3 * l + 0, hT, TT, hT)
                if NOSSM:
                    k.memset(oA[:, :, :], 0.0)
                else:
                    win_u(l, i)
                    ssm_tile(l, i)
                stop_at(7)
                win_phase(l, i)
                stop_at(2)
                attention(l, i)
                stop_at(3)
                merge_wout(l, hT, oA, oB, oC, xt, TT, merged, sa, acc)
                stop_at(4)
                rmsnorm(xt, 3 * l + 1, hT, TT, hT)
                mlp(l, hT, xt, TT, hid, tmpb)
                stop_at(5)
                rmsnorm(xt, 3 * l + 2, hT, TT, hT)
                ple(l, hT, xt, TT, pT[l][:, tsl], pbuf, sg, tmpf)
                stop_at(6)
                if l < L - 1:
                    k.dma("sp", xmid[:, tsl].rearrange("(k p) t -> p k t", p=128), xt[:, :, :])
                else:
                    rmsnorm(xt, 6, ytmp, TT, hT)
                    k.dma("sp", yT[:, tsl].rearrange("(k p) t -> p k t", p=128), ytmp[:, :, :])
            if not NOSAMPLE:
                sample_layer(l)
    except _Stop:
        pass
    k.finish()
    return nc, k


_CACHE = {}


def _get_nc(cfg):
    key = tuple(sorted((a, b) for a, b in cfg.items() if not a.startswith("_")))
    if key not in _CACHE:
        _CACHE[key] = build(cfg)
    return _CACHE[key]


WNAMES = ["w_in", "g_mix", "g_mlp", "g_ple", "g_final", "ssm_a_re", "ssm_a_im", "ssm_log_dt", "ssm_b_re", "ssm_b_im",
          "ssm_c_re", "ssm_c_im", "ssm_d", "w_glu", "b_glu", "attn_sinks", "w_branch_a", "w_branch_b", "w_branch_c",
          "w_out", "w_up", "w_down", "w_ple_gate", "w_ple_proj"]


def kernel(**inp):
    cfg = dict(CFG)
    inp = {a: np.asarray(b) for a, b in inp.items()}
    B, SEQ, _ = inp["x_prompt"].shape
    cfg["SEQ"] = SEQ
    cfg["NCORES"] = B
    NS = inp["x_sample"].shape[0] // B
    cfg["NS"] = NS
    L = cfg["DEPTH"]
    nc, _k = _get_nc(cfg)
    cst = host_consts()
    f = lambda a: np.ascontiguousarray(a, dtype=np.float32)
    in_maps = []
    for c in range(B):
        sl = slice(c * NS, (c + 1) * NS)
        m = {
            "xT": f(inp["x_prompt"][c].T),
            "pT": f(inp["p_prompt"][:, c].transpose(0, 2, 1)),
            "xsT": f(inp["x_sample"][sl, 0].T),
            "psT": f(inp["p_sample"][:, sl, 0].transpose(0, 2, 1)),
            "c_swa": f(inp["cache_swa_kv"][:, sl]),
            "c_d1": f(inp["cache_dil_d1_kv"][:, sl]),
            "c_d4": f(inp["cache_dil_d4_kv"][:, sl]),
            "c_d16": f(inp["cache_dil_d16_kv"][:, sl]),
            "st_ssm": f(inp["state_ssm"][:, sl].reshape(L, NS, 4096)),
            "cst": cst,
        }
        for w in WNAMES:
            m[w] = f(inp[w])
        in_maps.append(m)
    res = run_bass_kernel_spmd(nc, in_maps, core_ids=list(range(B)))
    R = res.results
    keep = [min(w, SEQ) for w in KEEP]
    y_p = np.stack([R[c]["yT"].T for c in range(B)]).astype(np.float32)
    y_s = np.concatenate([R[c]["ysT"].T for c in range(B)])[:, None, :].astype(np.float32)
    nh = [2, 4, 4, 4]
    outs_p = []
    for gi, nm in enumerate(["o_swa", "o_d1", "o_d4", "o_d16"]):
        a = np.stack([R[c][nm] for c in range(B)], axis=1)
        outs_p.append(a.reshape(L, B, keep[gi], 2, nh[gi], 64).astype(np.float32))
    ssm_p = np.stack([R[c]["o_ssm"] for c in range(B)], axis=1).reshape(L, B, 32, 64, 2).astype(np.float32)
    outs_s = []
    for gi, nm in enumerate(["s_swa", "s_d1", "s_d4", "s_d16"]):
        a = np.concatenate([R[c][nm] for c in range(B)], axis=1)
        outs_s.append(a.reshape(L, B * NS, 1, 2, nh[gi], 64).astype(np.float32))
    ssm_s = np.concatenate([R[c]["s_ssm"] for c in range(B)], axis=1).reshape(L, B * NS, 32, 64, 2).astype(np.float32)
    kernel.last_results = R
    return (y_p, y_s, outs_p[0], outs_p[1], outs_p[2], outs_p[3], ssm_p,
            outs_s[0], outs_s[1], outs_s[2], outs_s[3], ssm_s)
```

```python
import numpy as np
import concourse.bass as bass
import concourse.mybir as mybir
from concourse.bass_utils import run_bass_kernel_spmd

F32 = mybir.dt.float32
BF16 = mybir.dt.bfloat16
AF = mybir.ActivationFunctionType
ALU = mybir.AluOpType
AX = mybir.AxisListType

CFG = {"SEQ": 4096, "NCORES": 8, "NS": 16, "DEPTH": 2, "DEBUG": None}

D = 1024
KC = 8
TT = 512
DIN = 6656
DFF = 4096
PLE = 256
EPS = 1e-6
DILS = (1, 4, 16)
KEEP = (128, 128, 512, 2048)
NSLOT = 3


def _es(dt):
    return 2 if dt == BF16 else 4


class K:
    def __init__(self, nc):
        self.nc = nc
        self.eng = {"pe": nc.tensor, "act": nc.scalar, "dve": nc.vector, "pool": nc.gpsimd, "sp": nc.sync}
        self.sem = {e: nc.semaphore("prog_" + e).__enter__() for e in self.eng}
        self.cnt = {e: 0 for e in self.eng}
        self.seen = {e: {} for e in self.eng}
        self.recs = {}
        self.dsem = {}
        self.dtot = {}
        self.notrack = set()
        self.ps = [nc.psum_tensor("psb%d" % i, [128, 512], F32).__enter__() for i in range(8)]
        self.psi = 0
        self.flip = 0
        self.nwait = 0
        self.dsem_c = {}
        self.batch = None

    def box(self, ap):
        es = _es(ap.dtype)
        steps = ap.ap
        if "DRAM" in str(ap.space).upper():
            lo = ap.offset
            hi = lo + sum(st * (c - 1) for st, c in steps)
            return (ap.name, 0, 1, lo * es, (hi + 1) * es)
        pstep, pcnt = steps[0]
        p0 = ap.offset // pstep if pstep > 0 else 0
        f0 = ap.offset - p0 * pstep
        hi = f0 + sum(st * (c - 1) for st, c in steps[1:])
        if ap.name.startswith("psb"):
            return (ap.name, 0, 128, 0, 1 << 30)
        return (ap.name, p0, p0 + pcnt, f0 * es, (hi + 1) * es)

    def _collect(self, ap, is_write, need, skip_key=None, eng=None):
        name, p0, p1, b0, b1 = self.box(ap)
        if name in self.notrack:
            return
        psum = name.startswith("psb")
        for r in self.recs.get(name, ()):
            if r[0] < p1 and p0 < r[1] and r[2] < b1 and b0 < r[3]:
                if (not is_write) and r[4] == "r" and (not psum or r[5] == eng):
                    continue
                if skip_key is not None and r[5] == skip_key:
                    continue
                v = r[6] if isinstance(r[6], int) else r[6][0]
                if v is None:
                    continue
                if need.get(r[5], 0) < v:
                    need[r[5]] = v

    def _record(self, ap, kind, key, val):
        name, p0, p1, b0, b1 = self.box(ap)
        if name in self.notrack:
            return
        lst = self.recs.setdefault(name, [])
        if kind == "w":
            lst[:] = [r for r in lst if not (p0 <= r[0] and r[1] <= p1 and b0 <= r[2] and r[3] <= b1)]
        else:
            lst[:] = [r for r in lst if not (r[4] == "r" and r[5] == key and p0 <= r[0] and r[1] <= p1
                                             and b0 <= r[2] and r[3] <= b1)]
        lst.append((p0, p1, b0, b1, kind, key, val))

    def _semof(self, key):
        return self.sem[key] if key in self.sem else self.dsem[key]

    def _wait(self, e, need):
        for key, val in need.items():
            if self.seen[e].get(key, 0) < val:
                self.eng[e].wait_ge(self._semof(key), val)
                self.seen[e][key] = val
                self.nwait += 1

    def issue(self, e, fn, reads=(), writes=(), force=False):
        need = {}
        sk = "pe" if (e == "pe" and not force) else None
        for ap in reads:
            self._collect(ap, False, need, sk, e)
        for ap in writes:
            self._collect(ap, True, need, sk, e)
        self._wait(e, need)
        ins = fn()
        self.cnt[e] += 1
        ins.then_inc(self.sem[e], 1)
        c = self.cnt[e]
        for ap in reads:
            self._record(ap, "r", e, c)
        for ap in writes:
            self._record(ap, "w", e, c)
        return ins

    NDSEM = 96

    def _next_dsem(self, q="sp"):
        pre, n = ("sq", 12) if q == "pool" else ("dq", 28)
        i = self.dsem_c.get(pre, 0) % n
        self.dsem_c[pre] = self.dsem_c.get(pre, 0) + 1
        key = "%s%d" % (pre, i)
        if key not in self.dsem:
            self.dsem[key] = self.nc.semaphore(key).__enter__()
            self.dtot[key] = 0
        return key

    def batch_begin(self, q="sp"):
        assert self.batch is None
        key = self._next_dsem(q)
        self.batch = {"key": key, "cell": [None], "first": True}

    def batch_end(self):
        self.batch["cell"][0] = self.dtot[self.batch["key"]]
        self.batch = None

    def dma(self, q, out, in_, tag=None, **kw):
        need = {}
        self._collect(in_, False, need)
        self._collect(out, True, need)
        if self.batch is None:
            key = self._next_dsem(q)
            first = True
        else:
            key = self.batch["key"]
            first = self.batch["first"]
            self.batch["first"] = False
        if first and self.dtot[key] > 0:
            need[key] = max(need.get(key, 0), self.dtot[key])
        self._wait(q, need)
        ins = self.eng[q].dma_start(out=out, in_=in_, **kw)
        ins.then_inc(self.dsem[key], 16)
        self.dtot[key] += 16
        val = self.dtot[key] if self.batch is None else self.batch["cell"]
        self._record(in_, "r", key, val)
        self._record(out, "w", key, val)

    def finish(self):
        need = {t: v for t, v in self.dtot.items() if v > 0}
        for e in ("pe", "act", "dve", "pool"):
            if self.cnt[e] > 0:
                need[e] = self.cnt[e]
        self._wait("sp", need)

    def psum(self):
        t = self.ps[self.psi]
        self.psi = (self.psi + 1) % 8
        return t

    def mm(self, out, lhsT, rhs, start=True, stop=True, force=False):
        return self.issue("pe", lambda: self.nc.tensor.matmul(out, lhsT, rhs, start=start, stop=stop),
                          reads=[lhsT, rhs], writes=[out], force=force)

    def tr(self, out, in_, ident):
        return self.issue("pe", lambda: self.nc.tensor.transpose(out, in_, ident), reads=[in_, ident], writes=[out])

    def act(self, out, in_, func, bias=None, scale=None):
        kw = {}
        rd = [in_]
        if bias is not None:
            kw["bias"] = bias
            if not isinstance(bias, (int, float)):
                rd.append(bias)
        if scale is not None:
            kw["scale"] = scale
            if not isinstance(scale, (int, float)):
                rd.append(scale)
        return self.issue("act", lambda: self.nc.scalar.activation(out=out, in_=in_, func=func, **kw),
                          reads=rd, writes=[out])

    def tt(self, out, in0, in1, op, eng="dve"):
        return self.issue(eng, lambda: self.eng[eng].tensor_tensor(out=out, in0=in0, in1=in1, op=op),
                          reads=[in0, in1], writes=[out])

    def ts(self, out, in0, s1, s2, op0, op1=None, eng="dve"):
        rd = [in0] + [s for s in (s1, s2) if s is not None and not isinstance(s, (int, float))]
        if op1 is None:
            f = lambda: self.eng[eng].tensor_scalar(out=out, in0=in0, scalar1=s1, scalar2=None, op0=op0)
        else:
            f = lambda: self.eng[eng].tensor_scalar(out=out, in0=in0, scalar1=s1, scalar2=s2, op0=op0, op1=op1)
        return self.issue(eng, f, reads=rd, writes=[out])

    def stt(self, out, in0, scalar, in1, op0, op1):
        rd = [in0, in1] + ([scalar] if not isinstance(scalar, (int, float)) else [])
        return self.issue("dve", lambda: self.nc.vector.scalar_tensor_tensor(
            out=out, in0=in0, scalar=scalar, in1=in1, op0=op0, op1=op1), reads=rd, writes=[out])

    def copy(self, out, in_, eng=None):
        if eng is None:
            self.flip ^= 1
            eng = "act" if self.flip else "dve"
        if eng == "act":
            return self.issue("act", lambda: self.nc.scalar.copy(out=out, in_=in_), reads=[in_], writes=[out])
        return self.issue(eng, lambda: self.eng[eng].tensor_copy(out=out, in_=in_), reads=[in_], writes=[out])

    def recip(self, out, in_):
        return self.issue("dve", lambda: self.nc.vector.reciprocal(out=out, in_=in_), reads=[in_], writes=[out])

    def recip_act(self, out, in_):
        self.act(out, in_, AF.Ln)
        return self.act(out, out, AF.Exp, scale=-1.0)

    def memset(self, ap, v, eng="dve"):
        return self.issue(eng, lambda: self.eng[eng].memset(ap, v), reads=[], writes=[ap])

    def scan(self, out, d0, d1, init, op0=ALU.mult, op1=ALU.add):
        rd = [d0, d1] + ([init] if not isinstance(init, (int, float)) else [])
        return self.issue("dve", lambda: self.nc.vector.tensor_tensor_scan(
            out=out, data0=d0, data1=d1, initial=init, op0=op0, op1=op1), reads=rd, writes=[out])

    def reduce(self, out, in_, op, axis=AX.X):
        return self.issue("dve", lambda: self.nc.vector.tensor_reduce(out=out, in_=in_, axis=axis, op=op),
                          reads=[in_], writes=[out])


def host_consts():
    c = np.zeros((128, 1024), np.float32)
    kk = np.arange(128)[:, None]
    qq = np.arange(128)[None, :]
    c[:, 0:128] = np.eye(128, dtype=np.float32)
    c[:, 128:256] = (kk <= qq)
    c[:, 256:384] = (kk >= qq)
    c[:, 384:512] = ((qq // 16) >= (kk // 16))
    for i in range(8):
        for s in range(4):
            t = [t for t in range(i - 4, i) if t % 4 == s][0]
            k32 = np.arange(32)[:, None]
            q32 = np.arange(32)[None, :]
            ok = (t >= 0) & ((32 * t + k32) >= (32 * i + q32 - 128))
            c[32 * s:32 * s + 32, 512 + 32 * i:512 + 32 * i + 32] = ok
    k32 = np.arange(32)[:, None]
    q32 = np.arange(32)[None, :]
    for s in range(4):
        c[32 * s:32 * s + 32, 768:800] = (k32 <= q32)
    return c


def build(cfg):
    SEQ, NS, L = cfg["SEQ"], cfg["NS"], cfg["DEPTH"]
    NT = SEQ // TT
    NOSSM = cfg.get("NOSSM", False)
    NOSAMPLE = cfg.get("NOSAMPLE", False)
    keep = [min(w, SEQ) for w in KEEP]
    nc = bass.Bass("TRN2", target_bir_lowering=False, dynamic_dma_scratch_size=8192)
    k = K(nc)

    def din(name, shape):
        k.notrack.add(name)
        return nc.dram_tensor(name, list(shape), F32, kind="ExternalInput").ap()

    def dout(name, shape):
        k.notrack.add(name)
        return nc.dram_tensor(name, list(shape), F32, kind="ExternalOutput").ap()

    def sb(name, shape, dt):
        return nc.sbuf_tensor(name, list(shape), dt).__enter__()

    xT = din("xT", [D, SEQ])
    pT = din("pT", [L, PLE, SEQ])
    xsT = din("xsT", [D, NS])
    psT = din("psT", [L, PLE, NS])
    c_swa = din("c_swa", [L, NS, 128, 2, 2, 64])
    c_dil = [din("c_d1", [L, NS, 128, 2, 4, 64]), din("c_d4", [L, NS, 512, 2, 4, 64]),
             din("c_d16", [L, NS, 2048, 2, 4, 64])]
    st_ssm = din("st_ssm", [L, NS, 4096])
    w_in = din("w_in", [L, D, DIN])
    g_mix = din("g_mix", [L, D]); g_mlp = din("g_mlp", [L, D]); g_ple = din("g_ple", [L, D])
    g_final = din("g_final", [D])
    a_re = din("ssm_a_re", [L, 32, 64]); a_im = din("ssm_a_im", [L, 32, 64]); log_dt = din("ssm_log_dt", [L, 32])
    b_re = din("ssm_b_re", [L, 32, 64, 16]); b_im = din("ssm_b_im", [L, 32, 64, 16])
    c_re = din("ssm_c_re", [L, 32, 16, 64]); c_im = din("ssm_c_im", [L, 32, 16, 64])
    ssm_d = din("ssm_d", [L, 512]); w_glu = din("w_glu", [L, 512, 512]); b_glu = din("b_glu", [L, 512])
    sinks = din("attn_sinks", [L, 8])
    w_ba = din("w_branch_a", [L, 512, D]); w_bb = din("w_branch_b", [L, 512, D]); w_bc = din("w_branch_c", [L, 256, D])
    w_out = din("w_out", [L, D, D]); w_up = din("w_up", [L, D, DFF]); w_down = din("w_down", [L, DFF, D])
    w_pg = din("w_ple_gate", [L, D, D]); w_pp = din("w_ple_proj", [L, PLE, D])
    cst_d = din("cst", [128, 1024])

    yT = dout("yT", [D, SEQ]); ysT = dout("ysT", [D, NS])
    o_kv = [dout("o_swa", [L, keep[0], 256]), dout("o_d1", [L, keep[1], 512]),
            dout("o_d4", [L, keep[2], 512]), dout("o_d16", [L, keep[3], 512])]
    o_ssm = dout("o_ssm", [L, 4096])
    s_kv = [dout("s_swa", [L, NS, 256]), dout("s_d1", [L, NS, 512]), dout("s_d4", [L, NS, 512]),
            dout("s_d16", [L, NS, 512])]
    s_ssm = dout("s_ssm", [L, NS, 4096])
    xmid = nc.dram_tensor("xmid", [D, SEQ], F32, kind="Internal").ap()
    dbg = dout("dbg", [128, 4096]) if cfg.get("DEBUG") else None

    xt = sb("xt", [128, KC, TT], F32)
    hT = sb("hT", [128, KC, TT], BF16)
    wslots = [sb("wslot%d" % i, [128, 4096], BF16) for i in range(NSLOT)]
    kSd = sb("kSd", [128, 2, 2, TT], BF16)
    vS = sb("vS", [128, 8, 128], BF16)
    kD0 = sb("kD0", [128, 2, 2, TT], BF16)
    vD0 = sb("vD0", [128, 8, 256], BF16)
    kD1 = sb("kD1", [128, 2, 2, TT], BF16)
    vD1 = sb("vD1", [128, 2, 4, 256], BF16)
    kD2 = sb("kD2", [128, 2, SEQ], BF16)
    vD2 = sb("vD2", [128, 16, 256], BF16)
    kR = [kD0, kD1, kD2]
    kRn = [2, 2, 1]
    cstf = sb("cst_sb", [128, 256], F32)
    cb = sb("cstb", [128, 416], BF16)
    ones_b = sb("ones_b", [128, 128], BF16)
    ones_f = sb("ones_f", [128, 128], F32)
    gv = sb("gv", [128, 7, 8], F32)
    bglu = sb("bglu", [128, L, 4], F32)
    dcol = sb("dcol", [128, L, 4], F32)
    esink = sb("esink", [128, L * 8], F32)
    xs = sb("xs", [128, KC, NS], F32)
    hs = sb("hs", [128, KC, NS], BF16)
    AW = cfg.get("AW", 11776)
    arena = sb("arena", [128, AW], F32)

    def av(off, shape, dt):
        nel = 1
        for s_ in shape[1:]:
            nel *= s_
        nb = nel * _es(dt)
        assert off % 4 == 0 and off + nb <= AW * 4, (off, nb, AW * 4)
        ap = arena[:, off // 4:(off + nb + 3) // 4]
        if dt != F32:
            ap = ap.bitcast(dt)
        if len(shape) == 3:
            ap = ap.rearrange("p (a b) -> p a b", a=shape[1])
        elif len(shape) == 4:
            ap = ap.rearrange("p (a b c) -> p a b c", a=shape[1], b=shape[2])
        if shape[0] < 128:
            ap = ap[0:shape[0]]
        return ap

    ident_f = cstf[:, 0:128]
    ident_b = cb[:, 0:128]
    mcur = cb[:, 128:256]
    mprev = cb[:, 256:384]
    mT0 = cstf[:, 128:256]
    rows = av(0, [128, 512], F32)[0:1]
    mcur16 = cb[:, 384:416]

    ncd = dict(allow_slow_non_contiguous=True)

    cst_tmp = av(2048, [128, 1024], F32)
    k.dma("sp", cst_tmp[:, :], cst_d[:, :])
    k.copy(cb[:, 0:384], cst_tmp[:, 0:384], eng="dve")
    k.copy(cb[:, 384:416], cst_tmp[:, 768:800], eng="dve")
    k.copy(cstf[:, 0:128], cst_tmp[:, 0:128], eng="dve")
    k.copy(cstf[:, 128:256], cst_tmp[:, 384:512], eng="dve")
    k.memset(ones_b[:, :], 1.0)
    k.memset(ones_f[:, :], 1.0)
    epsb = sb("epsb", [128, 1], F32)
    k.memset(epsb[:, :], EPS)
    k.batch_begin()
    for l in range(L):
        for j, g in enumerate((g_mix, g_mlp, g_ple)):
            k.dma("sp", gv[:, 3 * l + j, :], g[l].rearrange("(k p) -> p k", p=128), **ncd)
        k.dma("sp", bglu[:, l, :], b_glu[l].rearrange("(k p) -> p k", p=128), **ncd)
        k.dma("sp", dcol[:, l, :], ssm_d[l].rearrange("(k p) -> p k", p=128), **ncd)
        k.dma("sp", rows[0:1, 8 * l:8 * l + 8], sinks[l:l + 1, :])
    k.dma("sp", gv[:, 6, :], g_final.rearrange("(k p) -> p k", p=128), **ncd)
    k.batch_end()
    ps = k.psum()
    k.mm(ps[:, 0:8 * L], ones_f[0:1, :], rows[0:1, 0:8 * L])
    k.act(esink[:, :], ps[:, 0:8 * L], AF.Exp)

    slot_i = [0]

    def wload(W, r0, nrows, c0, ncols):
        kc_ = nrows // 128
        s_ = wslots[slot_i[0] % NSLOT]
        slot_i[0] += 1
        view = s_[:, 0:kc_ * ncols].rearrange("p (k n) -> p k n", k=kc_)
        src = W[r0:r0 + nrows, c0:c0 + ncols].rearrange("(k p) n -> p k n", p=128)
        k.dma("pool", view, src)
        return view

    def rmsnorm(x3, gi, out3, N, tmp3):
        for kk in range(KC):
            k.act(tmp3[:, kk, :], x3[:, kk, :], AF.Square)
        ps = k.psum()
        for kk in range(KC):
            k.mm(ps[:, 0:N], ones_b[:, :], tmp3[:, kk, :], start=kk == 0, stop=kk == KC - 1)
        rs = av(RS_OFF, [128, TT], F32)[:, 0:N]
        k.act(rs, ps[:, 0:N], AF.Ln, bias=epsb[:, 0:1], scale=1.0 / D)
        k.act(rs, rs, AF.Exp, scale=-0.5)
        for kk in range(KC):
            k.stt(out3[:, kk, :], x3[:, kk, :], gv[:, gi, kk:kk + 1], rs, ALU.mult, ALU.mult)

    RS_OFF = 0
    Q_S = 2048
    Q_D = Q_S + 4096
    O_A = Q_D + 6144
    O_B = O_A + 4096
    O_C = O_B + 4096
    PH = O_C + 2048
    PT_P = [PH, PH + 1024]
    PT_C = [PH + 2048, PH + 3072]
    RD = PH + 4096
    ACCN = RD + 2048
    ACCD = ACCN + 4096
    VC16 = ACCD + 8192
    OST = VC16 + 3072
    ATT_END = OST + 4096
    MRG = PH
    SA = MRG + 8192
    ACC = SA + 2048
    HID = Q_S
    YST = Q_S
    SG = Q_S + 32768

    qS = av(Q_S, [128, 4, TT], BF16)
    qD = av(Q_D, [128, 6, TT], BF16)
    oA = av(O_A, [128, 4, TT], BF16)
    oB = av(O_B, [128, 4, TT], BF16)
    oC = av(O_C, [128, 2, TT], BF16)
    ptp = [av(o, [128, TT], BF16) for o in PT_P]
    ptc = [av(o, [128, TT], BF16) for o in PT_C]
    rd = av(RD, [128, TT], F32)
    accN = av(ACCN, [128, 2, TT], F32)
    accD = av(ACCD, [128, 4, TT], F32)
    vc16 = av(VC16, [128, 6, 256], BF16)
    ost = [av(OST, [128, 512], F32), av(OST, [128, 512], F32)]
    ost_i = [0]

    def stage():
        ost_i[0] ^= 1
        return ost[ost_i[0]]

    def bc_heads(m, nh, nq):
        return m.unsqueeze(1).to_broadcast([m.shape[0], nh, nq])

    blk_i = [0]

    def qk_phase(nh, nq, qf, prev_segs, cur_k, cp0, pmask, cmask, split=False):
        b_ = blk_i[0] % 2
        blk_i[0] += 1
        st = {"nh": nh, "nq": nq, "cp0": cp0, "prev": bool(prev_segs)}
        npar = 2 if split else 1

        def bank_col(h):
            return (h % 2, (h // 2) * nq) if split else (0, h * nq)

        def exp_out(PT, rows, pss):
            v3 = PT[rows, 0:nh * nq].rearrange("p (h q) -> p h q", h=nh)
            if split:
                for par in range(2):
                    k.act(v3[:, par:nh:2, :], pss[par][rows, 0:(nh // 2) * nq].rearrange("p (h q) -> p h q", h=nh // 2),
                          AF.Exp, scale=0.125)
            else:
                k.act(PT[rows, 0:nh * nq], pss[0][rows, 0:nh * nq], AF.Exp, scale=0.125)
            return v3
        if prev_segs:
            psP = [k.psum() for _ in range(npar)]
            npk = sum(s_[2] for s_ in prev_segs)
            st["npk"] = npk
            for h in range(nh):
                bk, c0 = bank_col(h)
                for (kf, p0, nk) in prev_segs:
                    k.mm(psP[bk][p0:p0 + nk, c0:c0 + nq], kf(h), qf(h))
            PTp = ptp[b_]
            v3 = exp_out(PTp, slice(0, npk), psP)
            k.tt(v3, v3, bc_heads(pmask[0:npk], nh, nq), ALU.mult)
            st["PTp"] = PTp
        psC = [k.psum() for _ in range(npar)]
        for h in range(nh):
            bk, c0 = bank_col(h)
            k.mm(psC[bk][cp0:cp0 + nq, c0:c0 + nq], cur_k(h), qf(h))
        PTc = ptc[b_]
        v3 = exp_out(PTc, slice(cp0, cp0 + nq), psC)
        k.tt(v3, v3, bc_heads(cmask, nh, nq), ALU.mult)
        st["PTc"] = PTc
        return st

    def pv_phase(st, prev_v, cur_v, finish):
        nh, nq, cp0 = st["nh"], st["nq"], st["cp0"]
        psN = k.psum()
        psD = k.psum()
        for h in range(nh):
            out = psN[64 * (h % 2):64 * (h % 2) + 64, (h // 2) * nq:(h // 2 + 1) * nq]
            if st["prev"]:
                k.mm(out, prev_v(h), st["PTp"][0:st["npk"], h * nq:(h + 1) * nq], start=True, stop=False)
            k.mm(out, cur_v(h), st["PTc"][cp0:cp0 + nq, h * nq:(h + 1) * nq], start=not st["prev"], stop=True,
                 force=(st["prev"] and st["npk"] < 128))
        if st["prev"]:
            k.mm(psD[:, 0:nh * nq], ones_b[0:st["npk"], :], st["PTp"][0:st["npk"], 0:nh * nq], start=True, stop=False)
        k.mm(psD[:, 0:nh * nq], ones_b[cp0:cp0 + nq, :], st["PTc"][cp0:cp0 + nq, 0:nh * nq],
             start=not st["prev"], stop=True, force=(st["prev"] and st["npk"] < 128))
        finish(psN, psD)

    def run_pipelined(blocks):
        prev = None
        for qa, pa in blocks:
            st = qk_phase(*qa)
            if prev is not None:
                pv_phase(prev[0], *prev[1])
            prev = (st, pa)
        if prev is not None:
            pv_phase(prev[0], *prev[1])

    def attention(l, i):
        sl = i % 2
        blocks = []
        hpf0 = lambda h: slice(64 * (h % 2), 64 * (h % 2) + 64)
        for b in range(4):
            for hg in range(2):
                qf = lambda h, b=b, hg=hg: qS[hpf0(h), 2 * hg + h // 2, 128 * b:128 * b + 128]
                if b > 0:
                    segs = [(lambda h, b=b, hg=hg: kSd[hpf0(h), hg, sl, 128 * (b - 1):128 * b], 0, 128)]
                    pv_ = lambda h, b=b, hg=hg: vS[:, 4 * sl + b - 1, 64 * hg:64 * hg + 64]
                elif i > 0:
                    segs = [(lambda h, hg=hg: kSd[hpf0(h), hg, 1 - sl, 384:512], 0, 128)]
                    pv_ = lambda h, hg=hg: vS[:, 4 * (1 - sl) + 3, 64 * hg:64 * hg + 64]
                else:
                    segs = []
                    pv_ = None
                ck = lambda h, b=b, hg=hg: kSd[hpf0(h), hg, sl, 128 * b:128 * b + 128]
                cv = lambda h, b=b, hg=hg: vS[:, 4 * sl + b, 64 * hg:64 * hg + 64]

                def fin(psN, psD, b=b, hg=hg):
                    for h in range(4):
                        H = 4 * hg + h
                        k.ts(rd[:, 128 * h:128 * h + 128], psD[:, 128 * h:128 * h + 128],
                             esink[:, 8 * l + H:8 * l + H + 1], None, ALU.add)
                    k.recip_act(rd[:, :], rd[:, :])
                    for h in range(4):
                        H = 4 * hg + h
                        pp = slice(64 * (h % 2), 64 * (h % 2) + 64)
                        k.tt(oB[pp, H // 2, 128 * b:128 * b + 128], psN[pp, (h // 2) * 128:(h // 2) * 128 + 128],
                             rd[pp, 128 * h:128 * h + 128], ALU.mult)
                blocks.append(((4, 128, qf, segs, ck, 0, mprev, mcur, True), (pv_, cv, fin)))
        for gi, d in enumerate(DILS):
            kr = kR[gi]
            nsl = kRn[gi]
            s_c = i % nsl if gi < 2 else 0
            if d == 1:
                subs = [(0, b) for b in range(4)]
            elif d == 4:
                subs = [(r, 0) for r in range(4)]
            else:
                subs = [(r, 0) for r in range(16)]
            for (r, b) in subs:
                nq = 128 if d < 16 else 32
                cp0 = 0 if d < 16 else 32 * (r % 3)
                t0 = 128 * b + r
                tsl = slice(t0, t0 + d * (nq - 1) + 1, d)
                hpf = lambda h: slice(64 * (h % 2), 64 * (h % 2) + 64)
                qf = lambda h, gi=gi, tsl=tsl: qD[hpf(h), 2 * gi + h // 2, tsl]
                if gi < 2:
                    ck = lambda h, kr=kr, s_c=s_c, tsl=tsl: kr[hpf(h), h // 2, s_c, tsl]
                else:
                    ck = lambda h, r=r: kD2[hpf(h), h // 2, TT * i + r:TT * i + TT:16]
                if d == 1:
                    cv = lambda h, b=b: vD0[:, 4 * sl + b, 64 * h:64 * h + 64]
                    if b > 0:
                        segs = [(lambda h, kr=kr, s_c=s_c, b=b: kr[hpf(h), h // 2, s_c, 128 * (b - 1):128 * b], 0, 128)]
                        pv_ = lambda h, b=b: vD0[:, 4 * sl + b - 1, 64 * h:64 * h + 64]
                    elif i > 0:
                        segs = [(lambda h, kr=kr, s_c=s_c: kr[hpf(h), h // 2, 1 - s_c, 384:512], 0, 128)]
                        pv_ = lambda h: vD0[:, 4 * (1 - sl) + 3, 64 * h:64 * h + 64]
                    else:
                        segs, pv_ = [], None
                    pm, cm = mprev, mcur
                elif d == 4:
                    cv = lambda h, r=r: vD1[:, sl, r, 64 * h:64 * h + 64]
                    if i > 0:
                        segs = [(lambda h, kr=kr, s_c=s_c, tsl=tsl: kr[hpf(h), h // 2, 1 - s_c, tsl], 0, 128)]
                        pv_ = lambda h, r=r: vD1[:, 1 - sl, r, 64 * h:64 * h + 64]
                    else:
                        segs, pv_ = [], None
                    pm, cm = mprev, mcur
                else:
                    cv = lambda h, r=r, cp0=cp0: vc16[cp0:cp0 + 32, r // 3, 64 * h:64 * h + 64]
                    nw = min(i, 4)
                    if nw > 0:
                        t0_ = TT * (i - nw)
                        segs = [(lambda h, r=r, t0_=t0_: kD2[hpf(h), h // 2, t0_ + r:TT * i:16], 0, 32 * nw)]
                        pv_ = lambda h, r=r, nw=nw: vD2[0:32 * nw, r, 64 * h:64 * h + 64]
                    else:
                        segs, pv_ = [], None
                    pm = mprev[:, 0:32] if i >= 4 else ones_b[:, 0:32]
                    cm = mcur16[cp0:cp0 + 32, :]

                def fin(psN, psD, gi=gi, tsl=tsl, nq=nq):
                    n3 = psN[:, 0:2 * nq].rearrange("p (c q) -> p c q", c=2)
                    d3 = psD[:, 0:4 * nq].rearrange("p (c q) -> p c q", c=4)
                    if gi == 0:
                        k.copy(accN[:, :, tsl], n3)
                        k.copy(accD[:, :, tsl], d3)
                    else:
                        k.tt(accN[:, :, tsl], n3, accN[:, :, tsl], ALU.add)
                        k.tt(accD[:, :, tsl], d3, accD[:, :, tsl], ALU.add)
                blocks.append(((4, nq, qf, segs, ck, cp0, pm, cm, True), (pv_, cv, fin)))
        ATT = cfg.get("ATT", 15)
        blocks = [b_ for b_, g_ in zip(blocks, [0] * 8 + [1] * 4 + [2] * 4 + [3] * 16) if (ATT >> g_) & 1]
        run_pipelined(blocks)
        k.recip_act(accD[:, :, :], accD[:, :, :])
        for h in range(4):
            pp = slice(64 * (h % 2), 64 * (h % 2) + 64)
            k.tt(oC[pp, h // 2, :], accN[pp, h // 2, :], accD[pp, h, :], ALU.mult)
        if i >= 4:
            for a in range(3):
                k.dma("sp", vD2[32 * a:32 * a + 32, :, :], vD2[32 * a + 32:32 * a + 64, :, :])
        q4 = min(i, 3)
        for rq in range(3):
            n_ = len(range(rq, 16, 3))
            k.dma("sp", vD2[32 * q4:32 * q4 + 32, rq:16:3, :], vc16[32 * rq:32 * rq + 32, 0:n_, :])

    def win_phase(l, i):
        W = w_in[l]
        last = (i == NT - 1)
        sl = i % 2
        hk = lambda kk: hT[:, kk, :]

        def fm(wv, c0, evac, lhs=None):
            ps = k.psum()
            for kk in range(KC):
                lt = lhs(kk) if lhs is not None else wv[:, kk, c0:c0 + 128]
                k.mm(ps[:, :], lt, hk(kk), start=kk == 0, stop=kk == KC - 1)
            evac(ps)

        def tok(wv, c0, ncols, lhs_cols, M, evac):
            ps = k.psum()
            for kk in range(KC):
                k.mm(ps[0:M, 0:ncols], hT[:, kk, lhs_cols], wv[:, kk, c0:c0 + ncols], start=kk == 0, stop=kk == KC - 1)
            evac(ps)

        w1 = wload(W, 0, D, 512, 512)
        for c in range(4):
            fm(w1, 128 * c, lambda ps, c=c: k.copy(qS[:, c, :], ps[:, :]))
        w2 = wload(W, 0, D, 1024, 512)
        def ev_ks(ps):
            k.copy(kSd[0:64, 0, sl, :], ps[0:64, :], "dve")
            k.copy(kSd[64:128, 1, sl, :], ps[64:128, :], "dve")
            k.batch_begin()
            k.dma("sp", kSd[64:128, 0, sl, :], kSd[0:64, 0, sl, :])
            k.dma("sp", kSd[0:64, 1, sl, :], kSd[64:128, 1, sl, :])
            k.batch_end()
        fm(w2, 0, ev_ks)
        for j in range(4):
            if last and j == 3:
                def ev(ps, j=j):
                    st_ = stage()
                    k.copy(st_[:, 0:256], ps[:, 0:256])
                    k.copy(vS[:, 4 * sl + j, :], ps[:, 128:256])
                    k.dma("sp", o_kv[0][l, :, :], st_[:, 0:256])
                tok(w2, 0, 256, slice(128 * j, 128 * j + 128), 128, ev)
            else:
                tok(w2, 128, 128, slice(128 * j, 128 * j + 128), 128,
                    lambda ps, j=j: k.copy(vS[:, 4 * sl + j, :], ps[:, 0:128]))
        for c in range(2):
            fm(w2, 256 + 128 * c, lambda ps, c=c: k.copy(qD[:, c, :], ps[:, :]))
        w3 = wload(W, 0, D, 1536, 512)
        for c in range(2):
            fm(w3, 128 * c, lambda ps, c=c: k.copy(kD0[:, c, sl, :], ps[:, :]))
        for j in range(4):
            if last and j == 3:
                def ev(ps, j=j):
                    st_ = stage()
                    k.copy(st_[:, :], ps[:, :])
                    k.copy(vD0[:, 4 * sl + j, :], ps[:, 256:512])
                    k.dma("sp", o_kv[1][l, :, :], st_[:, :])
                tok(w3, 0, 512, slice(128 * j, 128 * j + 128), 128, ev)
            else:
                tok(w3, 256, 256, slice(128 * j, 128 * j + 128), 128,
                    lambda ps, j=j: k.copy(vD0[:, 4 * sl + j, :], ps[:, 0:256]))
        w4 = wload(W, 0, D, 2048, 512)
        for c in range(2):
            fm(w4, 128 * c, lambda ps, c=c: k.copy(qD[:, 2 + c, :], ps[:, :]))
        for c in range(2):
            fm(w4, 256 + 128 * c, lambda ps, c=c: k.copy(kD1[:, c, sl, :], ps[:, :]))
        if last:
            for r in range(4):
                def ev(ps, r=r):
                    st_ = stage()
                    k.copy(st_[:, 0:256], ps[:, 0:256])
                    k.dma("sp", o_kv[2][l, r:keep[2]:4, 0:256], st_[:, 0:256])
                tok(w4, 256, 256, slice(r, 512, 4), 128, ev)
        w5 = wload(W, 0, D, 2560, 512)
        for r in range(4):
            def ev(ps, r=r):
                k.copy(vD1[:, sl, r, :], ps[:, 0:256])
                if last:
                    st_ = stage()
                    k.copy(st_[:, 0:256], ps[:, 0:256])
                    k.dma("sp", o_kv[2][l, r:keep[2]:4, 256:512], st_[:, 0:256])
            tok(w5, 0, 256, slice(r, 512, 4), 128, ev)
        for c in range(2):
            fm(w5, 256 + 128 * c, lambda ps, c=c: k.copy(qD[:, 4 + c, :], ps[:, :]))
        w6 = wload(W, 0, D, 3072, 512)
        for c in range(2):
            fm(w6, 128 * c, lambda ps, c=c: k.copy(kD2[:, c, TT * i:TT * i + TT], ps[:, :]))
        for r in range(16):
            p0 = 32 * (r % 3)
            ps = k.psum()
            for kk in range(KC):
                k.mm(ps[p0:p0 + 32, 0:256], hT[:, kk, r:512:16], w6[:, kk, 256:512], start=kk == 0, stop=kk == KC - 1)
            k.copy(vc16[p0:p0 + 32, r // 3, :], ps[p0:p0 + 32, 0:256])
        ntail = keep[3] // TT
        if i >= NT - ntail:
            row0 = (i - (NT - ntail)) * TT
            for j in range(4):
                def ev(ps, j=j):
                    st_ = stage()
                    k.copy(st_[:, :], ps[:, :])
                    k.dma("sp", o_kv[3][l, row0 + 128 * j:row0 + 128 * j + 128, :], st_[:, :])
                tok(w6, 0, 512, slice(128 * j, 128 * j + 128), 128, ev)

    def wload_pack(items):
        s_ = wslots[slot_i[0] % NSLOT]
        slot_i[0] += 1
        off = 0
        views = []
        for (W, r0, nrows, c0, ncols) in items:
            kc_ = nrows // 128
            view = s_[:, off:off + kc_ * ncols].rearrange("p (k n) -> p k n", k=kc_)
            off += kc_ * ncols
            assert off <= 4096
            k.dma("pool", view, W[r0:r0 + nrows, c0:c0 + ncols].rearrange("(k p) n -> p k n", p=128))
            views.append(view)
        return views

    def merge_wout(l, hT_, oA_, oB_, oC_, x_, N, merged, sa, acc):
        W = w_in[l]
        wbr = [w_ba[l], w_bb[l], w_bc[l]]
        nkb = [4, 4, 2]
        ob = [oA_, oB_, oC_]
        for mq in range(4):
            for b_ in range(3):
                wg, wb = wload_pack([(W, 0, D, 3584 + 1024 * b_ + 256 * mq, 256),
                                     (wbr[b_], 0, 128 * nkb[b_], 256 * mq, 256)])
                for mm_ in range(2):
                    m = 2 * mq + mm_
                    ps = k.psum()
                    for kk in range(KC):
                        k.mm(ps[:, 0:N], wg[:, kk, 128 * mm_:128 * mm_ + 128], hT_[:, kk, 0:N],
                             start=kk == 0, stop=kk == KC - 1)
                    k.act(sa[:, 0:N], ps[:, 0:N], AF.Sigmoid)
                    ps2 = k.psum()
                    for kk in range(nkb[b_]):
                        k.mm(ps2[:, 0:N], wb[:, kk, 128 * mm_:128 * mm_ + 128], ob[b_][:, kk, 0:N],
                             start=kk == 0, stop=kk == nkb[b_] - 1)
                    if b_ == 0:
                        k.tt(acc[:, mm_, 0:N], ps2[:, 0:N], sa[:, 0:N], ALU.mult)
                    else:
                        k.tt(sa[:, 0:N], ps2[:, 0:N], sa[:, 0:N], ALU.mult)
                        if b_ == 1:
                            k.tt(acc[:, mm_, 0:N], acc[:, mm_, 0:N], sa[:, 0:N], ALU.add)
                        else:
                            k.tt(merged[:, m, 0:N], acc[:, mm_, 0:N], sa[:, 0:N], ALU.add)
        for mg in range(2):
            wo = wload(w_out[l], 0, D, 512 * mg, 512)
            for mm_ in range(4):
                m = 4 * mg + mm_
                ps = k.psum()
                for kk in range(KC):
                    k.mm(ps[:, 0:N], wo[:, kk, 128 * mm_:128 * mm_ + 128], merged[:, kk, 0:N],
                         start=kk == 0, stop=kk == KC - 1)
                k.tt(x_[:, m, 0:N], ps[:, 0:N], x_[:, m, 0:N], ALU.add)

    def mlp(l, hT_, x_, N, hid, tmp):
        for ug in range(8):
            wu = wload(w_up[l], 0, D, 512 * ug, 512)
            for mm_ in range(4):
                j = 4 * ug + mm_
                ps = k.psum()
                for kk in range(KC):
                    k.mm(ps[:, 0:N], wu[:, kk, 128 * mm_:128 * mm_ + 128], hT_[:, kk, 0:N],
                         start=kk == 0, stop=kk == KC - 1)
                k.act(tmp[:, 0:N], ps[:, 0:N], AF.Relu)
                k.tt(hid[:, j, 0:N], tmp[:, 0:N], tmp[:, 0:N], ALU.mult)
        for mg in range(2):
            pss = [k.psum() for _ in range(4)]
            for ku in range(4):
                wd = wload(w_down[l], 1024 * ku, 1024, 512 * mg, 512)
                for mm_ in range(4):
                    for kk in range(KC):
                        k.mm(pss[mm_][:, 0:N], wd[:, kk, 128 * mm_:128 * mm_ + 128], hid[:, 8 * ku + kk, 0:N],
                             start=(ku == 0 and kk == 0), stop=(ku == 3 and kk == KC - 1))
            for mm_ in range(4):
                m = 4 * mg + mm_
                k.tt(x_[:, m, 0:N], pss[mm_][:, 0:N], x_[:, m, 0:N], ALU.add)

    def ple(l, hT_, x_, N, psrc, pb, sg, tmp):
        k.dma("pool", pb[:, :, 0:N], psrc.rearrange("(k p) t -> p k t", p=128))
        for mg in range(2):
            wg = wload(w_pg[l], 0, D, 512 * mg, 512)
            wp = wload(w_pp[l], 0, PLE, 512 * mg, 512)
            for mm_ in range(4):
                m = 4 * mg + mm_
                ps = k.psum()
                for kk in range(KC):
                    k.mm(ps[:, 0:N], wg[:, kk, 128 * mm_:128 * mm_ + 128], hT_[:, kk, 0:N],
                         start=kk == 0, stop=kk == KC - 1)
                k.act(sg[:, 0:N], ps[:, 0:N], AF.Sigmoid)
                ps2 = k.psum()
                for kk in range(2):
                    k.mm(ps2[:, 0:N], wp[:, kk, 128 * mm_:128 * mm_ + 128], pb[:, kk, 0:N], start=kk == 0, stop=kk == 1)
                k.tt(tmp[:, 0:N], ps2[:, 0:N], sg[:, 0:N], ALU.mult)
                k.tt(x_[:, m, 0:N], x_[:, m, 0:N], tmp[:, 0:N], ALU.add)

    merged = av(MRG, [128, 8, TT], BF16)
    sa = av(SA, [128, TT], F32)
    acc = av(ACC, [128, 2, TT], F32)
    hid = av(HID, [128, 32, TT], BF16)
    ytmp = av(YST, [128, 8, TT], F32)
    sg = av(SG, [128, TT], F32)
    tmpf = av(SG + 2048, [128, TT], F32)
    tmpb = av(SG + 4096, [128, TT], BF16)
    pbuf = av(Q_S, [128, 2, TT], BF16)

    Win = sb("Win", [128, 32, 128], BF16)
    Wo = sb("Wo", [128, 16, 2, 128], BF16)
    T0 = sb("T0w", [128, 32, 128], BF16)
    Ct = sb("Ct", [128, 16, 64], F32)
    St = sb("St", [128, 16, 64], F32)
    r8p = sb("r8p", [128, 16], F32)
    SbB = sb("SbB", [128, 2, 16, 65], BF16)
    carry = sb("carry", [128, 16, 2], F32)
    lam1 = sb("lam1", [128, 2, 16], F32)
    Drep = sb("Drep", [128, 512], F32)
    SUB, MUL, ADD = ALU.subtract, ALU.mult, ALU.add

    def ssm_setup(l):
        off = [2048]

        def T(shape=(128, 16), dt=F32):
            shape = list(shape)
            nel = 1
            for s_ in shape[1:]:
                nel *= s_
            ap = av(off[0], shape, dt)
            off[0] += ((nel * _es(dt) + 3) // 4) * 4
            return ap
        Are, Aim = T(), T()
        k.batch_begin()
        for gg in range(2):
            hp = slice(64 * gg, 64 * gg + 64)
            k.dma("sp", Are[hp, :], a_re[l, gg:32:2, :].rearrange("j p -> p j"), **ncd)
            k.dma("sp", Aim[hp, :], a_im[l, gg:32:2, :].rearrange("j p -> p j"), **ncd)
        k.dma("sp", rows[0:1, 0:32], log_dt[l:l + 1, :])
        k.batch_end()
        ps = k.psum()
        k.mm(ps[:, 0:32], ones_f[0:1, :], rows[0:1, 0:32])
        dt_ = T()
        k.act(dt_[0:64, :], ps[0:64, 0:32:2], AF.Exp)
        k.act(dt_[64:128, :], ps[64:128, 1:32:2], AF.Exp)
        th, rr = T(), T()
        k.tt(th, Aim, dt_, MUL)
        k.tt(rr, Are, dt_, MUL)
        rho1, irho1 = T(), T()
        k.act(rho1, rr, AF.Exp)
        k.act(irho1, rr, AF.Exp, scale=-1.0)
        phi, p2, acc_, sn, cs, t1, t2 = T(), T(), T(), T(), T(), T(), T()
        k.ts(phi, th, 0.125, None, MUL)
        k.tt(p2, phi, phi, MUL)
        k.memset(acc_, 1.0)
        for n in range(9, 0, -1):
            k.tt(acc_, acc_, p2, MUL)
            k.ts(acc_, acc_, -1.0 / ((2 * n) * (2 * n + 1)), 1.0, MUL, ADD)
        k.tt(sn, acc_, phi, MUL)
        k.memset(acc_, 1.0)
        for n in range(9, 0, -1):
            k.tt(acc_, acc_, p2, MUL)
            k.ts(acc_, acc_, -1.0 / ((2 * n - 1) * (2 * n)), 1.0, MUL, ADD)
        k.copy(cs, acc_, "dve")
        for _ in range(3):
            k.tt(t1, cs, cs, MUL)
            k.tt(t2, sn, sn, MUL)
            k.tt(sn, sn, cs, MUL)
            k.ts(sn, sn, 2.0, None, MUL)
            k.tt(cs, t1, t2, SUB)
        lr, li, ir, ii = T(), T(), T(), T()
        k.tt(lr, rho1, cs, MUL)
        k.tt(li, rho1, sn, MUL)
        k.tt(ir, irho1, cs, MUL)
        k.tt(ii, irho1, sn, MUL)
        k.ts(ii, ii, -1.0, None, MUL)
        k.copy(lam1[:, 0, :], lr, "dve")
        k.copy(lam1[:, 1, :], li, "dve")
        den, lm1, zr, zi = T(), T(), T(), T()
        k.tt(den, Are, Are, MUL)
        k.tt(t1, Aim, Aim, MUL)
        k.tt(den, den, t1, ADD)
        k.recip(den, den)
        k.ts(lm1, lr, -1.0, None, ADD)
        k.tt(zr, lm1, Are, MUL)
        k.tt(t1, li, Aim, MUL)
        k.tt(zr, zr, t1, ADD)
        k.tt(zr, zr, den, MUL)
        k.tt(zi, li, Are, MUL)
        k.tt(t1, lm1, Aim, MUL)
        k.tt(zi, zi, t1, SUB)
        k.tt(zi, zi, den, MUL)
        S3 = (128, 16, 16)
        Bre, Bim, Bbr, Bbi, u1, u2 = [T(S3) for _ in range(6)]
        Cre, Cim = Bre, Bim
        k.batch_begin()
        for gg in range(2):
            hp = slice(64 * gg, 64 * gg + 64)
            k.dma("sp", Bre[hp, :, :], b_re[l, gg:32:2].rearrange("j p h -> p j h"), **ncd)
            k.dma("sp", Bim[hp, :, :], b_im[l, gg:32:2].rearrange("j p h -> p j h"), **ncd)
        k.batch_end()
        bz = lambda z: z.unsqueeze(2).to_broadcast([128, 16, 16])
        k.tt(u1, Bre, bz(zr), MUL)
        k.tt(u2, Bim, bz(zi), MUL)
        k.tt(Bbr, u1, u2, SUB)
        k.tt(u1, Bim, bz(zr), MUL)
        k.tt(u2, Bre, bz(zi), MUL)
        k.tt(Bbi, u1, u2, ADD)
        k.batch_begin()
        for gg in range(2):
            hp = slice(64 * gg, 64 * gg + 64)
            for j in range(16):
                k.dma("sp", Cre[hp, j, :], c_re[l, 2 * j + gg].rearrange("h p -> p h"), **ncd)
                k.dma("sp", Cim[hp, j, :], c_im[l, 2 * j + gg].rearrange("h p -> p h"), **ncd)
        k.batch_end()
        LPr, LPi = T((128, 9, 16)), T((128, 9, 16))
        LQr, LQi = T((128, 8, 16)), T((128, 8, 16))
        for (Pr_, Pi_, xr_, xi_, n_) in ((LPr, LPi, lr, li, 9), (LQr, LQi, ir, ii, 8)):
            k.memset(Pr_[:, 0, :], 1.0)
            k.memset(Pi_[:, 0, :], 0.0)
            for kk in range(1, n_):
                k.tt(t1, Pr_[:, kk - 1, :], xr_, MUL)
                k.tt(t2, Pi_[:, kk - 1, :], xi_, MUL)
                k.tt(Pr_[:, kk, :], t1, t2, SUB)
                k.tt(t1, Pr_[:, kk - 1, :], xi_, MUL)
                k.tt(t2, Pi_[:, kk - 1, :], xr_, MUL)
                k.tt(Pi_[:, kk, :], t1, t2, ADD)
        S4 = (128, 16, 128)
        WTr, WTi, WPr, WPi = T(S4), T(S4), T(S4), T(S4)
        bp = lambda P, kk: P[:, kk, :].unsqueeze(2).to_broadcast([128, 16, 16])
        for a in range(8):
            sl_ = slice(16 * a, 16 * a + 16)
            k.tt(u1, Bbr, bp(LPr, 7 - a), MUL)
            k.tt(u2, Bbi, bp(LPi, 7 - a), MUL)
            k.tt(WTr[:, :, sl_], u1, u2, SUB)
            k.tt(u1, Bbr, bp(LPi, 7 - a), MUL)
            k.tt(u2, Bbi, bp(LPr, 7 - a), MUL)
            k.tt(WTi[:, :, sl_], u1, u2, ADD)
            k.tt(u1, Cre, bp(LPr, a + 1), MUL)
            k.tt(u2, Cim, bp(LPi, a + 1), MUL)
            k.tt(Wo[:, :, 0, sl_], u1, u2, SUB)
            k.tt(u1, Cre, bp(LPi, a + 1), MUL)
            k.tt(u2, Cim, bp(LPr, a + 1), MUL)
            k.tt(u1, u1, u2, ADD)
            k.ts(Wo[:, :, 1, sl_], u1, -1.0, None, MUL)
            k.tt(u1, Cre, bp(LQr, 7 - a), MUL)
            k.tt(u2, Cim, bp(LQi, 7 - a), MUL)
            k.tt(WPr[:, :, sl_], u1, u2, SUB)
            k.tt(u1, Cre, bp(LQi, 7 - a), MUL)
            k.tt(u2, Cim, bp(LQr, 7 - a), MUL)
            k.tt(u1, u1, u2, ADD)
            k.ts(WPi[:, :, sl_], u1, -1.0, None, MUL)
        for j in range(16):
            for ri, WT in enumerate((WTr, WTi)):
                ps = k.psum()
                k.tr(ps[:, 0:128], WT[:, j, :], ident_f)
                k.copy(Win[:, 2 * j:2 * j + 2, 64 * ri:64 * ri + 64], ps[:, 0:128].rearrange("p (g q) -> p g q", g=2))
        for j in range(16):
            for gg in range(2):
                hp = slice(64 * gg, 64 * gg + 64)
                ps = k.psum()
                k.mm(ps[:, 0:128], WTr[hp, j, :], WPr[hp, j, :], start=True, stop=False)
                k.mm(ps[:, 0:128], WTi[hp, j, :], WPi[hp, j, :], start=False, stop=True)
                k.tt(T0[:, 2 * j + gg, :], ps[:, 0:128], mT0, MUL)
        E1r, E1i, r8i, r8, Pr, Pi = T(), T(), T(), T(), T(), T()
        k.act(r8i, rr, AF.Exp, scale=-8.0)
        k.act(r8, rr, AF.Exp, scale=8.0)
        k.tt(E1r, LPr[:, 8, :], r8i, MUL)
        k.tt(E1i, LPi[:, 8, :], r8i, MUL)
        k.copy(r8p[:, :], r8, "dve")
        k.copy(Ct[:, :, 0], E1r, "dve")
        k.copy(St[:, :, 0], E1i, "dve")
        k.copy(Pr, E1r, "dve")
        k.copy(Pi, E1i, "dve")
        v1, v2 = WTr[:, :, 0:32], WTr[:, :, 32:64]
        m = 1
        while m < 64:
            bP = lambda X, m=m: X.unsqueeze(2).to_broadcast([128, 16, m])
            k.tt(v1[:, :, 0:m], Ct[:, :, 0:m], bP(Pr), MUL)
            k.tt(v2[:, :, 0:m], St[:, :, 0:m], bP(Pi), MUL)
            k.tt(Ct[:, :, m:2 * m], v1[:, :, 0:m], v2[:, :, 0:m], SUB)
            k.tt(v1[:, :, 0:m], Ct[:, :, 0:m], bP(Pi), MUL)
            k.tt(v2[:, :, 0:m], St[:, :, 0:m], bP(Pr), MUL)
            k.tt(St[:, :, m:2 * m], v1[:, :, 0:m], v2[:, :, 0:m], ADD)
            k.tt(t1, Pr, Pr, MUL)
            k.tt(t2, Pi, Pi, MUL)
            k.tt(Pi, Pr, Pi, MUL)
            k.ts(Pi, Pi, 2.0, None, MUL)
            k.tt(Pr, t1, t2, SUB)
            m *= 2
        k.dma("sp", rows[0:1, 0:512], ssm_d[l:l + 1, :])
        ps = k.psum()
        k.mm(ps[:, 0:512], ones_f[0:1, :], rows[0:1, 0:512])
        k.copy(Drep[:, :], ps[:, :], "dve")
        k.memset(carry[:, :, :], 0.0)
        assert off[0] <= AW * 4, off[0]

    U_ = av(2048, [128, 32, 64], BF16)
    XT = [av(6144 + 1024 * n, [128, 4, 64], F32) for n in range(6)]
    V1p = av(16384, [128, 32, 128], BF16)
    Gtok = av(16384, [128, 8, 512], BF16)
    Ytok = av(24576, [128, 32, 128], F32)
    tg = av(40960, [128, 512], F32)
    gT = av(43008, [128, 4, TT], BF16)

    def win_u(l, i):
        w0 = wload(w_in[l], 0, D, 0, 512)
        for a in range(8):
            ps = k.psum()
            for kk in range(KC):
                k.mm(ps[0:64, :], hT[:, kk, a:512:8], w0[:, kk, :], start=kk == 0, stop=kk == KC - 1)
            k.copy(V1p[0:64, :, 16 * a:16 * a + 16], ps[0:64, :].rearrange("c (g h) -> c g h", g=32))

    def ssm_tile(l, i):
        for ri in range(2):
            k.copy(SbB[:, ri, :, 0], carry[:, :, ri], "dve")
        for gb in range(2):
            ps = k.psum()
            psb_ = ps[:, :].bitcast(BF16)
            for gl in range(16):
                g = 16 * gb + gl
                k.tr(psb_[:, 64 * gl:64 * gl + 64], V1p[0:64, g, :], ident_b[0:64, 0:64])
            k.copy(U_[:, 16 * gb:16 * gb + 16, :], psb_[:, :].rearrange("p (g c) -> p g c", g=16))
        for q in range(4):
            psr, psi_ = k.psum(), k.psum()
            for jj in range(4):
                j = 4 * q + jj
                for gg in range(2):
                    g = 2 * j + gg
                    hp = slice(64 * gg, 64 * gg + 64)
                    k.mm(psr[hp, 64 * jj:64 * jj + 64], Win[:, g, 0:64], U_[:, g, :])
                    k.mm(psi_[hp, 64 * jj:64 * jj + 64], Win[:, g, 64:128], U_[:, g, :])
            Xr = psr[:, 0:256].rearrange("p (j c) -> p j c", j=4)
            Xi = psi_[:, 0:256].rearrange("p (j c) -> p j c", j=4)
            Cq, Sq = Ct[:, 4 * q:4 * q + 4, :], St[:, 4 * q:4 * q + 4, :]
            a_, b_, xr, xi, wr, wi = XT
            k.tt(a_, Xr, Cq, MUL)
            k.tt(b_, Xi, Sq, MUL)
            k.tt(xr, a_, b_, ADD)
            k.tt(a_, Xi, Cq, MUL)
            k.tt(b_, Xr, Sq, MUL)
            k.tt(xi, a_, b_, SUB)
            for jj in range(4):
                j = 4 * q + jj
                rb = r8p[:, j:j + 1].to_broadcast([128, 64])
                k.scan(wr[:, jj, :], rb, xr[:, jj, :], carry[:, j, 0:1])
                k.scan(wi[:, jj, :], rb, xi[:, jj, :], carry[:, j, 1:2])
            k.tt(a_, wr, Cq, MUL)
            k.tt(b_, wi, Sq, MUL)
            k.tt(xr, a_, b_, SUB)
            k.tt(a_, wr, Sq, MUL)
            k.tt(b_, wi, Cq, MUL)
            k.tt(xi, a_, b_, ADD)
            k.copy(SbB[:, 0, 4 * q:4 * q + 4, 1:65], xr, "dve")
            k.copy(SbB[:, 1, 4 * q:4 * q + 4, 1:65], xi, "dve")
            k.copy(carry[:, 4 * q:4 * q + 4, 0], xr[:, :, 63], "dve")
            k.copy(carry[:, 4 * q:4 * q + 4, 1], xi[:, :, 63], "dve")
        Dv = Drep[0:64, :].rearrange("c (g h) -> c g h", g=32).unsqueeze(2).to_broadcast([64, 32, 8, 16])
        k.tt(Ytok[0:64, :, :].rearrange("c g (a h) -> c g a h", a=8),
             V1p[0:64, :, :].rearrange("c g (a h) -> c g a h", a=8), Dv, MUL)
        for q in range(4):
            pse, pso = k.psum(), k.psum()
            for jj in range(4):
                j = 4 * q + jj
                for gg, pb_ in ((0, pse), (1, pso)):
                    g = 2 * j + gg
                    hp = slice(64 * gg, 64 * gg + 64)
                    out = pb_[0:64, 128 * jj:128 * jj + 128]
                    k.mm(out, U_[:, g, :], T0[:, g, :], start=True, stop=False)
                    k.mm(out, SbB[hp, 0, j, 0:64], Wo[hp, j, 0, :], start=False, stop=False)
                    k.mm(out, SbB[hp, 1, j, 0:64], Wo[hp, j, 1, :], start=False, stop=True)
            for gg, pb_ in ((0, pse), (1, pso)):
                dst = Ytok[0:64, 8 * q + gg:8 * q + 8:2, :]
                k.tt(dst, pb_[0:64, 0:512].rearrange("c (j x) -> c j x", j=4), dst, ADD)
        for q in range(8):
            y = Ytok[0:64, 4 * q:4 * q + 4, :]
            t = tg[0:64, :].rearrange("c (g x) -> c g x", g=4)
            k.act(t, y, AF.Square)
            k.ts(t, t, 0.044715, 1.0, MUL, ADD)
            k.tt(t, t, y, MUL)
            k.act(t, t, AF.Sigmoid, scale=1.5957691216057308)
            out = Gtok[0:64, :, 64 * q:64 * q + 64].rearrange("c a (g h) -> c g a h", g=4)
            k.tt(out, y.rearrange("c g (a h) -> c g a h", a=8), t.rearrange("c g (a h) -> c g a h", a=8), MUL)
        for half in range(2):
            ps = k.psum()
            psb_ = ps[:, :].bitcast(BF16)
            for al in range(4):
                for fc in range(4):
                    o_ = 64 * (4 * al + fc)
                    k.tr(psb_[:, o_:o_ + 64], Gtok[0:64, 4 * half + al, 128 * fc:128 * fc + 128], ident_b[0:64, 0:64])
            for fc in range(4):
                out = gT[:, fc, :].rearrange("p (c a) -> p a c", a=8)[:, 4 * half:4 * half + 4, :]
                in_ = psb_[:, :].rearrange("p (a f c) -> p f a c", a=4, f=4)[:, fc]
                k.copy(out, in_)
        wgl = wload(w_glu[l], 0, 512, 0, 512)
        sa_ = tg[:, 0:512]
        for m in range(4):
            ps = k.psum()
            for kk in range(4):
                k.mm(ps[:, :], wgl[:, kk, 128 * m:128 * m + 128], gT[:, kk, :], start=kk == 0, stop=kk == 3)
            k.act(sa_, ps[:, :], AF.Sigmoid, bias=bglu[:, l, m:m + 1])
            k.tt(oA[:, m, :], gT[:, m, :], sa_, MUL)
        if i == NT - 1:
            for gg in range(2):
                k.dma("sp", o_ssm[l].rearrange("(j gg p ri) -> gg p j ri", j=16, gg=2, p=64)[gg],
                      carry[64 * gg:64 * gg + 64, :, :], **ncd)

    Us = sb("Us", [128, 32, NS], BF16)
    sinkp = sb("sinkp", [32, L, 4], F32)
    k.memset(Us[:, :, :], 0.0)
    k.batch_begin()
    for b in range(NS):
        for l_ in range(L):
            for kv in range(2):
                k.dma("sp", sinkp[NS * kv + b:NS * kv + b + 1, l_, :], sinks[l_:l_ + 1, 4 * kv:4 * kv + 4])
    k.batch_end()
    k.dma("sp", xs[:, :, :], xsT.rearrange("(k p) t -> p k t", p=128))

    def sample_layer(l):
        off = [2048]

        def T(shape, dt=F32):
            shape = list(shape)
            nel = 1
            for s_ in shape[1:]:
                nel *= s_
            ap = av(off[0], [128] + shape[1:], dt)
            off[0] += ((nel * _es(dt) + 3) // 4) * 4
            assert off[0] <= AW * 4, off[0]
            return ap[0:shape[0]] if shape[0] < 128 else ap
        ztok = T([NS, 3584])
        big0 = off[0]
        Hnat = T([NS, 4096])
        off[0] = big0
        KCH = 16
        kvb = [T([64, KCH, 64]), T([64, KCH, 64])]
        prod_off = off[0]
        prod = T([64, 32, 64])
        rmsnorm(xs, 3 * l + 0, hs, NS, hs)
        W = w_in[l]
        uTs = T([128, 4, NS], BF16)
        oAs, oBs, oCs = T([128, 4, NS], BF16), T([128, 4, NS], BF16), T([128, 2, NS], BF16)
        mark = off[0]
        for un in range(7):
            wv = wload(W, 0, D, 512 * un, 512)
            ps = k.psum()
            for kk in range(KC):
                k.mm(ps[0:NS, :], hs[:, kk, :], wv[:, kk, :], start=kk == 0, stop=kk == KC - 1)
            k.copy(ztok[:, 512 * un:512 * un + 512], ps[0:NS, :])
            if un == 0:
                for c in range(4):
                    ps2 = k.psum()
                    for kk in range(KC):
                        k.mm(ps2[:, 0:NS], wv[:, kk, 128 * c:128 * c + 128], hs[:, kk, :], start=kk == 0, stop=kk == KC - 1)
                    k.copy(uTs[:, c, :], ps2[:, 0:NS])
        k.dma("sp", s_kv[0][l, :, :], ztok[:, 1024:1280])
        for gi in range(3):
            c0 = 1280 + 768 * gi + 256
            k.dma("sp", s_kv[1 + gi][l, :, :], ztok[:, c0:c0 + 512])
        k.batch_begin()
        for g8 in range(8):
            k.dma("sp", Us[112:128, g8:32:8, :], uTs[16 * g8:16 * g8 + 16, 0:4, :])
        k.batch_end()
        k.dma("sp", Hnat[:, :], st_ssm[l, :, :])
        H0 = [T([128, 16, NS]), T([128, 16, NS])]
        H0b = [T([128, 16, NS], BF16), T([128, 16, NS], BF16)]
        for ri in range(2):
            ps = k.psum()
            for j in range(16):
                k.tr(ps[:, NS * j:NS * j + NS], Hnat[:, 256 * j + ri:256 * j + 256:2], ident_f[0:NS, 0:NS])
            k.copy(H0[ri][:, :, :], ps[:, 0:16 * NS].rearrange("p (j b) -> p j b", j=16), "dve")
            k.copy(H0b[ri][:, :, :], H0[ri][:, :, :], "dve")
        psx = [k.psum(), k.psum()]
        for j in range(16):
            for gg in range(2):
                g = 2 * j + gg
                hp = slice(64 * gg, 64 * gg + 64)
                for ri in range(2):
                    k.mm(psx[ri][hp, NS * j:NS * j + NS], Win[:, g, 64 * ri:64 * ri + 64], Us[:, g, :])
        sN = [T([128, 16, NS]), T([128, 16, NS])]
        ta, tb = T([128, 16, NS]), T([128, 16, NS])
        bl = lambda ri: lam1[:, ri, :].unsqueeze(2).to_broadcast([128, 16, NS])
        X3 = lambda ri: psx[ri][:, 0:16 * NS].rearrange("p (j b) -> p j b", j=16)
        k.tt(ta, H0[0], bl(0), MUL)
        k.tt(tb, H0[1], bl(1), MUL)
        k.tt(ta, ta, tb, SUB)
        k.tt(sN[0], X3(0), ta, ADD)
        k.tt(ta, H0[1], bl(0), MUL)
        k.tt(tb, H0[0], bl(1), MUL)
        k.tt(ta, ta, tb, ADD)
        k.tt(sN[1], X3(1), ta, ADD)
        Snat = Hnat
        for q in range(4):
            for ri in range(2):
                ps = k.psum()
                for jj in range(4):
                    j = 4 * q + jj
                    k.tr(ps[0:NS, 128 * jj:128 * jj + 128], sN[ri][:, j, :], ident_f)
                k.copy(Snat[:, 1024 * q + ri:1024 * q + 1024:2], ps[0:NS, :], "dve")
        k.dma("sp", s_ssm[l, :, :], Snat[:, :])
        psy = [k.psum(), k.psum()]
        for j in range(16):
            for gg in range(2):
                g = 2 * j + gg
                hp = slice(64 * gg, 64 * gg + 64)
                out = psy[gg][0:NS, 16 * j:16 * j + 16]
                k.mm(out, Us[:, g, :], T0[:, g, 112:128], start=True, stop=False)
                k.mm(out, H0b[0][hp, j, :], Wo[hp, j, 0, 0:16], start=False, stop=False)
                k.mm(out, H0b[1][hp, j, :], Wo[hp, j, 1, 0:16], start=False, stop=True)
        ys = T([NS, 512])
        tq = T([NS, 512])
        k.tt(ys, ztok[:, 0:512], Drep[0:NS, :], MUL)
        for gg in range(2):
            dst = ys.rearrange("b (j gg h) -> b j gg h", j=16, gg=2)[:, :, gg, :]
            k.tt(dst, psy[gg][0:NS, 0:256].rearrange("b (j h) -> b j h", j=16), dst, ADD)
        k.act(tq, ys, AF.Square)
        k.ts(tq, tq, 0.044715, 1.0, MUL, ADD)
        k.tt(tq, tq, ys, MUL)
        k.act(tq, tq, AF.Sigmoid, scale=1.5957691216057308)
        Gs = T([NS, 512], BF16)
        k.tt(Gs, ys, tq, MUL)
        gTs = T([128, 4, NS], BF16)
        ps = k.psum()
        psb_ = ps[:, :].bitcast(BF16)
        for fc in range(4):
            k.tr(psb_[:, NS * fc:NS * fc + NS], Gs[:, 128 * fc:128 * fc + 128], ident_b[0:NS, 0:NS])
        k.copy(gTs[:, :, :], psb_[:, 0:4 * NS].rearrange("p (f b) -> p f b", f=4))
        wgl = wload(w_glu[l], 0, 512, 0, 512)
        sas = T([128, NS])
        for m in range(4):
            ps = k.psum()
            for kk in range(4):
                k.mm(ps[:, 0:NS], wgl[:, kk, 128 * m:128 * m + 128], gTs[:, kk, :], start=kk == 0, stop=kk == 3)
            k.act(sas, ps[:, 0:NS], AF.Sigmoid, bias=bglu[:, l, m:m + 1])
            k.tt(oAs[:, m, :], gTs[:, m, :], sas, MUL)

        off[0] = mark
        def attn_core(nP, nqh, qsel, knew, vnew, cache, Lb, d, nh, sink_l):
            s = T([nP, nqh, 132])
            NCB = 128 // KCH
            for cb in range(NCB):
                Kc = kvb[cb % 2]
                k.batch_begin()
                for h in range(nh):
                    src = cache[l, :, (KCH * cb) * d:(KCH * cb + KCH) * d:d, 0, h, :]
                    k.dma("sp", Kc[NS * h:NS * h + NS, :, :], src)
                k.batch_end()
                for qh in range(nqh):
                    k.tt(prod[0:nP, 0:KCH], Kc[0:nP], qsel[:, qh, :].unsqueeze(1).to_broadcast([nP, KCH, 64]), MUL)
                    k.reduce(s[:, qh, KCH * cb:KCH * cb + KCH], prod[0:nP, 0:KCH], ALU.add)
            for qh in range(nqh):
                k.tt(prod[0:nP, 0, :], knew, qsel[:, qh, :], MUL)
                k.reduce(s[:, qh, 128:129], prod[0:nP, 0, :], ALU.add)
            m = T([nP, nqh])
            negm = T([nP, nqh])
            den = T([nP, nqh])
            k.reduce(m, s[:, :, 0:129], ALU.max)
            k.ts(negm, m, -0.125, None, MUL)
            p = T([nP, nqh, 132])
            for qh in range(nqh):
                k.act(p[:, qh, 0:129], s[:, qh, 0:129], AF.Exp, bias=negm[:, qh:qh + 1], scale=0.125)
            k.reduce(den, p[:, :, 0:129], ALU.add)
            if sink_l is not None:
                es = T([nP, nqh])
                for qh in range(nqh):
                    k.act(es[:, qh:qh + 1], sinkp[0:nP, sink_l, qh:qh + 1], AF.Exp, bias=negm[:, qh:qh + 1])
                k.tt(den, den, es, ADD)
            o = T([nP, nqh, 64])
            o2 = T([nP, 64])
            for qh in range(nqh):
                k.ts(o[:, qh, :], vnew, p[:, qh, 128:129], None, MUL)
            for cb in range(NCB):
                Vc = kvb[cb % 2]
                k.batch_begin()
                for h in range(nh):
                    src = cache[l, :, (KCH * cb) * d:(KCH * cb + KCH) * d:d, 1, h, :]
                    k.dma("sp", Vc[NS * h:NS * h + NS, :, :], src)
                k.batch_end()
                for qh in range(nqh):
                    k.tt(prod[0:nP, 0:KCH], Vc[0:nP],
                         p[:, qh, KCH * cb:KCH * cb + KCH].unsqueeze(2).to_broadcast([nP, KCH, 64]), MUL)
                    k.reduce(o2, prod[0:nP, 0:KCH].rearrange("p k d -> p d k"), ALU.add)
                    k.tt(o[:, qh, :], o[:, qh, :], o2, ADD)
            return o, m, den

        def to_bh(cols0, nP, nh, width=64):
            t_ = T([nP, width])
            k.batch_begin()
            for h in range(nh):
                k.dma("sp", t_[NS * h:NS * h + NS, :], ztok[:, cols0 + width * h:cols0 + width * h + width])
            k.batch_end()
            return t_
        nP = 2 * NS
        qs = T([nP, 4, 64])
        k.batch_begin()
        for kv in range(2):
            k.dma("sp", qs[NS * kv:NS * kv + NS, :, :],
                  ztok[:, 512 + 256 * kv:768 + 256 * kv].rearrange("b (q d) -> b q d", q=4))
        k.batch_end()
        kn = to_bh(1024, nP, 2)
        vn = to_bh(1152, nP, 2)
        o, m, den = attn_core(nP, 4, qs, kn, vn, c_swa, 128, 1, 2, l)
        k.recip(den, den)
        k.tt(o, o, den.unsqueeze(2).to_broadcast([nP, 4, 64]), MUL)
        otok = T([NS, 512])
        k.batch_begin()
        for kv in range(2):
            k.dma("sp", otok[:, 256 * kv:256 * kv + 256].rearrange("b (q d) -> b q d", q=4), o[NS * kv:NS * kv + NS, :, :])
        k.batch_end()
        otb = T([NS, 512], BF16)
        k.copy(otb, otok, "dve")
        ps = k.psum()
        psb_ = ps[:, :].bitcast(BF16)
        for fc in range(4):
            k.tr(psb_[:, NS * fc:NS * fc + NS], otb[:, 128 * fc:128 * fc + 128], ident_b[0:NS, 0:NS])
        k.copy(oBs[:, :, :], psb_[:, 0:4 * NS].rearrange("p (f b) -> p f b", f=4))
        nP = 4 * NS
        off[0] = mark
        res = []
        hold = [(T([nP, 1, 64]), T([nP, 1]), T([nP, 1])) for _ in range(3)]
        mark2 = off[0]
        for gi, d in enumerate(DILS):
            off[0] = mark2
            base = 1280 + 768 * gi
            qd = T([nP, 1, 64])
            k.batch_begin()
            for h in range(4):
                k.dma("sp", qd[NS * h:NS * h + NS, 0, :], ztok[:, base + 64 * h:base + 64 * h + 64])
            k.batch_end()
            kn = to_bh(base + 256, nP, 4)
            vn = to_bh(base + 512, nP, 4)
            o_, m_, d_ = attn_core(nP, 1, qd, kn, vn, c_dil[gi], 128 * d, d, 4, None)
            k.copy(hold[gi][0], o_, "dve")
            k.copy(hold[gi][1], m_, "dve")
            k.copy(hold[gi][2], d_, "dve")
            res.append(hold[gi])
        M = T([nP, 1])
        k.tt(M, res[0][1], res[1][1], ALU.max)
        k.tt(M, M, res[2][1], ALU.max)
        negM = T([nP, 1])
        k.ts(negM, M, -0.125, None, MUL)
        num = T([nP, 64])
        dsum = T([nP, 1])
        wg = T([nP, 1])
        tmpo = T([nP, 64])
        for gi in range(3):
            o, m, den = res[gi]
            k.act(wg, m, AF.Exp, bias=negM[:, 0:1], scale=0.125)
            if gi == 0:
                k.ts(num, o[:, 0, :], wg[:, 0:1], None, MUL)
                k.tt(dsum, den, wg, MUL)
            else:
                k.ts(tmpo, o[:, 0, :], wg[:, 0:1], None, MUL)
                k.tt(num, num, tmpo, ADD)
                k.tt(den, den, wg, MUL)
                k.tt(dsum, dsum, den, ADD)
        k.recip(dsum, dsum)
        k.ts(num, num, dsum[:, 0:1], None, MUL)
        otc = T([NS, 256])
        k.batch_begin()
        for h in range(4):
            k.dma("sp", otc[:, 64 * h:64 * h + 64], num[NS * h:NS * h + NS, :])
        k.batch_end()
        otcb = T([NS, 256], BF16)
        k.copy(otcb, otc, "dve")
        ps = k.psum()
        psb_ = ps[:, :].bitcast(BF16)
        for fc in range(2):
            k.tr(psb_[:, NS * fc:NS * fc + NS], otcb[:, 128 * fc:128 * fc + 128], ident_b[0:NS, 0:NS])
        k.copy(oCs[:, :, :], psb_[:, 0:2 * NS].rearrange("p (f b) -> p f b", f=2))
        mrg = T([128, 8, NS], BF16)
        sa_s = T([128, NS])
        acc_s = T([128, 2, NS])
        merge_wout(l, hs, oAs, oBs, oCs, xs, NS, mrg, sa_s, acc_s)
        rmsnorm(xs, 3 * l + 1, hs, NS, hs)
        hid_s = T([128, 32, NS], BF16)
        tb_s = T([128, NS], BF16)
        mlp(l, hs, xs, NS, hid_s, tb_s)
        rmsnorm(xs, 3 * l + 2, hs, NS, hs)
        pbs = T([128, 2, NS], BF16)
        sg_s = T([128, NS])
        tf_s = T([128, NS])
        ple(l, hs, xs, NS, psT[l], pbs, sg_s, tf_s)
        if l == L - 1:
            yts = T([128, KC, NS])
            rmsnorm(xs, 6, yts, NS, hs)
            k.dma("sp", ysT.rearrange("(k p) t -> p k t", p=128), yts[:, :, :])

    def ring_reset():
        for t_ in (kSd, vS, kD0, vD0, kD1, vD1, kD2, vD2):
            ap = t_[:]
            k.memset(ap, 0.0)

    SSM = cfg.get("_ssm")

    STOP = cfg.get("STOP", 0)

    class _Stop(Exception):
        pass

    def stop_at(n):
        if STOP == n:
            raise _Stop()
    try:
        for l in range(L):
            stop_at(1)
            ring_reset()
            if not NOSSM:
                ssm_setup(l)
            for i in range(NT):
                tsl = slice(TT * i, TT * i + TT)
                src_x = xT if l == 0 else xmid
                k.dma("sp", xt[:, :, :], src_x[:, tsl].rearrange("(k p) t -> p k t", p=128))
                rmsnorm(xt, 3 * l + 0, hT, TT, hT)
                if NOSSM:
                    k.memset(oA[:, :, :], 0.0)
                else:
                    win_u(l, i)
                    ssm_tile(l, i)
                stop_at(7)
                win_phase(l, i)
                stop_at(2)
                attention(l, i)
                stop_at(3)
                merge_wout(l, hT, oA, oB, oC, xt, TT, merged, sa, acc)
                stop_at(4)
                rmsnorm(xt, 3 * l + 1, hT, TT, hT)
                mlp(l, hT, xt, TT, hid, tmpb)
                stop_at(5)
                rmsnorm(xt, 3 * l + 2, hT, TT, hT)
                ple(l, hT, xt, TT, pT[l][:, tsl], pbuf, sg, tmpf)
                stop_at(6)
                if l < L - 1:
                    k.dma("sp", xmid[:, tsl].rearrange("(k p) t -> p k t", p=128), xt[:, :, :])
                else:
                    rmsnorm(xt, 6, ytmp, TT, hT)
                    k.dma("sp", yT[:, tsl].rearrange("(k p) t -> p k t", p=128), ytmp[:, :, :])
            if not NOSAMPLE:
                sample_layer(l)
    except _Stop:
        pass
    k.finish()
    return nc, k


_CACHE = {}


def _get_nc(cfg):
    key = tuple(sorted((a, b) for a, b in cfg.items() if not a.startswith("_")))
    if key not in _CACHE:
        _CACHE[key] = build(cfg)
    return _CACHE[key]


WNAMES = ["w_in", "g_mix", "g_mlp", "g_ple", "g_final", "ssm_a_re", "ssm_a_im", "ssm_log_dt", "ssm_b_re", "ssm_b_im",
          "ssm_c_re", "ssm_c_im", "ssm_d", "w_glu", "b_glu", "attn_sinks", "w_branch_a", "w_branch_b", "w_branch_c",
          "w_out", "w_up", "w_down", "w_ple_gate", "w_ple_proj"]


def kernel(**inp):
    cfg = dict(CFG)
    inp = {a: np.asarray(b) for a, b in inp.items()}
    B, SEQ, _ = inp["x_prompt"].shape
    cfg["SEQ"] = SEQ
    cfg["NCORES"] = B
    NS = inp["x_sample"].shape[0] // B
    cfg["NS"] = NS
    L = cfg["DEPTH"]
    nc, _k = _get_nc(cfg)
    cst = host_consts()
    f = lambda a: np.ascontiguousarray(a, dtype=np.float32)
    in_maps = []
    for c in range(B):
        sl = slice(c * NS, (c + 1) * NS)
        m = {
            "xT": f(inp["x_prompt"][c].T),
            "pT": f(inp["p_prompt"][:, c].transpose(0, 2, 1)),
            "xsT": f(inp["x_sample"][sl, 0].T),
            "psT": f(inp["p_sample"][:, sl, 0].transpose(0, 2, 1)),
            "c_swa": f(inp["cache_swa_kv"][:, sl]),
            "c_d1": f(inp["cache_dil_d1_kv"][:, sl]),
            "c_d4": f(inp["cache_dil_d4_kv"][:, sl]),
            "c_d16": f(inp["cache_dil_d16_kv"][:, sl]),
            "st_ssm": f(inp["state_ssm"][:, sl].reshape(L, NS, 4096)),
            "cst": cst,
        }
        for w in WNAMES:
            m[w] = f(inp[w])
        in_maps.append(m)
    res = run_bass_kernel_spmd(nc, in_maps, core_ids=list(range(B)))
    R = res.results
    keep = [min(w, SEQ) for w in KEEP]
    y_p = np.stack([R[c]["yT"].T for c in range(B)]).astype(np.float32)
    y_s = np.concatenate([R[c]["ysT"].T for c in range(B)])[:, None, :].astype(np.float32)
    nh = [2, 4, 4, 4]
    outs_p = []
    for gi, nm in enumerate(["o_swa", "o_d1", "o_d4", "o_d16"]):
        a = np.stack([R[c][nm] for c in range(B)], axis=1)
        outs_p.append(a.reshape(L, B, keep[gi], 2, nh[gi], 64).astype(np.float32))
    ssm_p = np.stack([R[c]["o_ssm"] for c in range(B)], axis=1).reshape(L, B, 32, 64, 2).astype(np.float32)
    outs_s = []
    for gi, nm in enumerate(["s_swa", "s_d1", "s_d4", "s_d16"]):
        a = np.concatenate([R[c][nm] for c in range(B)], axis=1)
        outs_s.append(a.reshape(L, B * NS, 1, 2, nh[gi], 64).astype(np.float32))
    ssm_s = np.concatenate([R[c]["s_ssm"] for c in range(B)], axis=1).reshape(L, B * NS, 32, 64, 2).astype(np.float32)
    kernel.last_results = R
    return (y_p, y_s, outs_p[0], outs_p[1], outs_p[2], outs_p[3], ssm_p,
            outs_s[0], outs_s[1], outs_s[2], outs_s[3], ssm_s)
```
